# Optimizing a Trainium2 kernel written in Bass

```python
import math
import jax, jax.numpy as jnp
from jax import lax
import numpy as np

D_MODEL = 1024
BATCH = 16
SEQ = 2048
DEPTH = 2

PLE_DIM = 256
N_HEADS = D_MODEL // 128
QK_HEAD_DIM = 64
V_HEAD_DIM = 2 * QK_HEAD_DIM
ROT_DIM = QK_HEAD_DIM // 4
ROPE_THETA = 500000.0
Q_BLOCK = 128
QK_WIDTH = N_HEADS * 2 * QK_HEAD_DIM
V_WIDTH = N_HEADS * V_HEAD_DIM
FNET_WIDTH = D_MODEL // 2
FNET_GROUPS = 4
FNET_GROUP_DIM = FNET_WIDTH // FNET_GROUPS
SSM_WIDTH = D_MODEL // 2
SSM_GROUP_DIM = 16
SSM_GROUPS = SSM_WIDTH // SSM_GROUP_DIM
SSM_STATE = 64
DT_MIN = 0.001
DT_MAX = 0.1
N_BRANCHES = 3
GATE_WIDTH = N_BRANCHES * D_MODEL
IN_SPLITS = (QK_WIDTH, 2 * QK_WIDTH, 2 * QK_WIDTH + V_WIDTH,
             2 * QK_WIDTH + V_WIDTH + FNET_WIDTH,
             2 * QK_WIDTH + V_WIDTH + FNET_WIDTH + SSM_WIDTH)
IN_WIDTH = 2 * QK_WIDTH + V_WIDTH + FNET_WIDTH + SSM_WIDTH + GATE_WIDTH
D_FF = ((8 * D_MODEL // 3 + 127) // 128) * 128
CONV_WIDTH = 3
POS_OFFSET_MAX = 1024
EPS = 1e-6

kernel_name = "hybrid_diffattn_fnet_s5_encoder"


def rms_norm(x, g):
    xf = x.astype(jnp.float32)
    y = xf * lax.rsqrt(jnp.mean(xf * xf, axis=-1, keepdims=True) + EPS)
    return (y * g.astype(jnp.float32)).astype(x.dtype)


def rope_tables(positions):
    inv_freq = ROPE_THETA ** (-jnp.arange(0, ROT_DIM, 2, dtype=jnp.float32) / ROT_DIM)
    ang = positions.astype(jnp.float32)[..., None] * inv_freq
    return jnp.cos(ang), jnp.sin(ang)


def partial_rotary(t, cos, sin):
    tf = t.astype(jnp.float32)
    rot, rest = tf[..., :ROT_DIM], tf[..., ROT_DIM:]
    half = ROT_DIM // 2
    r1, r2 = rot[..., :half], rot[..., half:]
    c = cos[:, :, None, None, :]
    s = sin[:, :, None, None, :]
    rot = jnp.concatenate([r1 * c - r2 * s, r2 * c + r1 * s], axis=-1)
    return jnp.concatenate([rot, rest], axis=-1).astype(t.dtype)


def diff_attention(q, k, v, lam):
    bsz, s_len, h, _, dh = q.shape
    nb = s_len // Q_BLOCK
    qb = (q * (dh ** -0.5)).reshape(bsz, nb, Q_BLOCK, h, 2, dh).transpose(1, 0, 2, 3, 4, 5)

    def block(qi):
        s = jnp.einsum('bqhcd,bkhcd->bhcqk', qi, k).astype(jnp.float32)
        pr = jax.nn.softmax(s, axis=-1)
        a = (pr[:, :, 0] - lam * pr[:, :, 1]).astype(v.dtype)
        return jnp.einsum('bhqk,bkhe->bqhe', a, v)

    o = lax.map(block, qb)
    return o.transpose(1, 0, 2, 3, 4).reshape(bsz, s_len, h, V_HEAD_DIM)


def fourier_mix(u):
    bsz, s_len, _ = u.shape
    ug = u.astype(jnp.float32).reshape(bsz, s_len, FNET_GROUPS, FNET_GROUP_DIM)
    f = jnp.fft.fftn(ug, axes=(1, 3), norm='ortho').real
    return f.reshape(bsz, s_len, FNET_WIDTH).astype(u.dtype)


def _ssm_combine(e1, e2):
    a1, b1 = e1
    a2, b2 = e2
    return a1 * a2, a2 * b1 + b2


def s5_mix(u, a_re, a_im, log_dt, b_re, b_im, c_re, c_im, d):
    bsz, s_len, _ = u.shape
    ug = u.astype(jnp.float32).reshape(bsz, s_len, SSM_GROUPS, SSM_GROUP_DIM)
    b_mat = lax.complex(b_re.astype(jnp.float32), b_im.astype(jnp.float32))
    c_mat = lax.complex(c_re.astype(jnp.float32), c_im.astype(jnp.float32))
    bu = jnp.einsum('bsgh,gph->bsgp', ug.astype(jnp.complex64), b_mat)
    state = jnp.zeros(bu.shape, jnp.complex64)
    for direction in range(2):
        lam = lax.complex(a_re[direction].astype(jnp.float32), a_im[direction].astype(jnp.float32))
        dt = jnp.exp(log_dt[direction].astype(jnp.float32))[:, None]
        a_bar = jnp.exp(lam * dt)
        b_scale = (a_bar - 1.0) / lam
        a_elems = jnp.broadcast_to(a_bar, (1, s_len, SSM_GROUPS, SSM_STATE))
        _, xs = lax.associative_scan(_ssm_combine, (a_elems, bu * b_scale),
                                     axis=1, reverse=(direction == 1))
        state = state + xs
    y = jnp.einsum('bsgp,ghp->bsgh', state, c_mat).real
    y = y + d.astype(jnp.float32).reshape(SSM_GROUPS, SSM_GROUP_DIM) * ug
    return y.reshape(bsz, s_len, SSM_WIDTH).astype(u.dtype)


def dwconv_centred(h, w, b):
    c = h.shape[-1]
    pad = CONV_WIDTH // 2
    out = lax.conv_general_dilated(h, w[:, None, :].astype(h.dtype), window_strides=(1,),
                                   padding=((pad, pad),),
                                   dimension_numbers=('NWC', 'WIO', 'NWC'),
                                   feature_group_count=c)
    return out + b.astype(h.dtype)


def setup_inputs(seed: int = 0) -> dict:
    key = jax.random.key(seed)
    ks = iter(jax.random.split(key, 48))
    f32 = jnp.float32
    L = DEPTH

    def nrm(shape, scale):
        return scale * jax.random.normal(next(ks), shape, f32)

    x = nrm((BATCH, SEQ, D_MODEL), 1.0)
    p = nrm((DEPTH, BATCH, SEQ, PLE_DIM), 1.0)
    positions = (jnp.arange(SEQ, dtype=jnp.int32)[None, :]
                 + jax.random.randint(next(ks), (BATCH, 1), 0, POS_OFFSET_MAX, dtype=jnp.int32))
    return {
        "x": x,
        "p": p,
        "positions": positions,
        "norm_mix_g": 1.0 + nrm((L, D_MODEL), 0.02),
        "w_in": nrm((L, D_MODEL, IN_WIDTH), D_MODEL ** -0.5),
        "lambda_q1": nrm((L, QK_HEAD_DIM), 0.1),
        "lambda_k1": nrm((L, QK_HEAD_DIM), 0.1),
        "lambda_q2": nrm((L, QK_HEAD_DIM), 0.1),
        "lambda_k2": nrm((L, QK_HEAD_DIM), 0.1),
        "attn_subln_g": 1.0 + nrm((L, V_HEAD_DIM), 0.02),
        "w_attn_br": nrm((L, V_WIDTH, D_MODEL), V_WIDTH ** -0.5),
        "w_fourier_br": nrm((L, FNET_WIDTH, D_MODEL), FNET_WIDTH ** -0.5),
        "ssm_a_re": -0.5 + nrm((L, 2, SSM_GROUPS, SSM_STATE), 0.01),
        "ssm_a_im": jnp.pi * jnp.arange(SSM_STATE, dtype=f32) + nrm((L, 2, SSM_GROUPS, SSM_STATE), 0.01),
        "ssm_log_dt": jax.random.uniform(next(ks), (L, 2, SSM_GROUPS), f32,
                                         math.log(DT_MIN), math.log(DT_MAX)),
        "ssm_b_re": nrm((L, SSM_GROUPS, SSM_STATE, SSM_GROUP_DIM), (2 * SSM_GROUP_DIM) ** -0.5),
        "ssm_b_im": nrm((L, SSM_GROUPS, SSM_STATE, SSM_GROUP_DIM), (2 * SSM_GROUP_DIM) ** -0.5),
        "ssm_c_re": nrm((L, SSM_GROUPS, SSM_GROUP_DIM, SSM_STATE), 0.5),
        "ssm_c_im": nrm((L, SSM_GROUPS, SSM_GROUP_DIM, SSM_STATE), 0.5),
        "ssm_d": 1.0 + nrm((L, SSM_WIDTH), 0.1),
        "w_glu": nrm((L, SSM_WIDTH, 2 * D_MODEL), SSM_WIDTH ** -0.5),
        "b_glu": nrm((L, 2 * D_MODEL), 0.01),
        "w_out": nrm((L, D_MODEL, D_MODEL), D_MODEL ** -0.5),
        "norm_ffn_g": 1.0 + nrm((L, D_MODEL), 0.02),
        "w_up": nrm((L, D_MODEL, 2 * D_FF), D_MODEL ** -0.5),
        "conv_w": nrm((L, CONV_WIDTH, D_FF), CONV_WIDTH ** -0.5),
        "conv_b": nrm((L, D_FF), 0.01),
        "w_down": nrm((L, D_FF, D_MODEL), D_FF ** -0.5),
        "norm_ple_g": 1.0 + nrm((L, D_MODEL), 0.02),
        "w_ple_gate": nrm((L, D_MODEL, D_MODEL), D_MODEL ** -0.5),
        "w_ple": nrm((L, PLE_DIM, D_MODEL), PLE_DIM ** -0.5),
        "final_norm_g": 1.0 + nrm((D_MODEL,), 0.02),
    }


def reference(x, p, positions, norm_mix_g, w_in, lambda_q1, lambda_k1, lambda_q2, lambda_k2,
              attn_subln_g, w_attn_br, w_fourier_br, ssm_a_re, ssm_a_im, ssm_log_dt,
              ssm_b_re, ssm_b_im, ssm_c_re, ssm_c_im, ssm_d, w_glu, b_glu, w_out,
              norm_ffn_g, w_up, conv_w, conv_b, w_down, norm_ple_g, w_ple_gate, w_ple,
              final_norm_g):
    bsz, s_len, _ = x.shape
    cos, sin = rope_tables(positions)
    for i in range(DEPTH):
        lam_init = 0.8 - 0.6 * math.exp(-0.3 * i)
        h = rms_norm(x, norm_mix_g[i])
        proj = h @ w_in[i]
        q, k, v, u_f, u_s, gates = jnp.split(proj, IN_SPLITS, axis=-1)

        q = partial_rotary(q.reshape(bsz, s_len, N_HEADS, 2, QK_HEAD_DIM), cos, sin)
        k = partial_rotary(k.reshape(bsz, s_len, N_HEADS, 2, QK_HEAD_DIM), cos, sin)
        v = v.reshape(bsz, s_len, N_HEADS, V_HEAD_DIM)
        lam = (jnp.exp(jnp.sum(lambda_q1[i].astype(jnp.float32) * lambda_k1[i].astype(jnp.float32)))
               - jnp.exp(jnp.sum(lambda_q2[i].astype(jnp.float32) * lambda_k2[i].astype(jnp.float32)))
               + lam_init)
        o = diff_attention(q, k, v, lam)
        o = rms_norm(o, attn_subln_g[i]) * (1.0 - lam_init)
        y_a = o.reshape(bsz, s_len, V_WIDTH) @ w_attn_br[i]

        y_b = fourier_mix(u_f) @ w_fourier_br[i]

        s_out = jax.nn.gelu(s5_mix(u_s, ssm_a_re[i], ssm_a_im[i], ssm_log_dt[i], ssm_b_re[i],
                                   ssm_b_im[i], ssm_c_re[i], ssm_c_im[i], ssm_d[i]))
        z = s_out @ w_glu[i] + b_glu[i]
        y_c = z[..., :D_MODEL] * jax.nn.sigmoid(z[..., D_MODEL:])

        g = jax.nn.sigmoid(gates)
        merged = (g[..., :D_MODEL] * y_a + g[..., D_MODEL:2 * D_MODEL] * y_b
                  + g[..., 2 * D_MODEL:] * y_c)
        x = x + merged @ w_out[i]

        h = rms_norm(x, norm_ffn_g[i])
        up = h @ w_up[i]
        gate = dwconv_centred(up[..., :D_FF], conv_w[i], conv_b[i])
        x = x + (jax.nn.gelu(gate) * up[..., D_FF:]) @ w_down[i]

        h = rms_norm(x, norm_ple_g[i])
        x = x + jax.nn.sigmoid(h @ w_ple_gate[i]) * (p[i] @ w_ple[i])
    return rms_norm(x, final_norm_g)
```

```python
import math
from contextlib import ExitStack
import numpy as np
import ml_dtypes
import concourse.bass as bass
import concourse.mybir as mybir
from concourse.bass_utils import run_bass_kernel_spmd

F32 = mybir.dt.float32
BF16 = mybir.dt.bfloat16
I32 = mybir.dt.int32
ALU = mybir.AluOpType
AF = mybir.ActivationFunctionType

S = 2048
D = 1024
NT = S // 128
EPS = 1e-6
TWO_PI = 2.0 * math.pi
SIN_SCALE = TWO_PI * (1.0 - 2e-6)


class Tile:
    __slots__ = ("name", "w", "r", "dsem", "dcnt", "excl")

    def __init__(self, name="", excl=False):
        self.name = name
        self.excl = excl
        self.w = {}
        self.r = {}
        self.dsem = None
        self.dcnt = 0


class Prog:
    ENGS = ("sync", "scalar", "vector", "gpsimd", "tensor")

    def __init__(self, nc):
        self.nc = nc
        self.st = ExitStack()
        self.scopes = [self.st]
        self.streams = {e: [] for e in self.ENGS}
        self.seen = {e: {} for e in self.ENGS}
        self.total = {}
        self.bar_seen = {}
        self.sems = {}
        for e in self.ENGS:
            self.sems[("e", e)] = self.st.enter_context(nc.semaphore("c_" + e))
            self.total[("e", e)] = 0
        self.sems["bar"] = self.st.enter_context(nc.semaphore("bar"))
        self.nbar = 0
        self.nsem = 0
        self.free_dsems = {}
        self.phase_tiles = []
        self.uid = 0
        self.verbose = False

    def push(self):
        s = ExitStack()
        self.scopes.append(s)
        return s

    def pop(self):
        self.scopes.pop().close()

    def sb(self, name, shape, dt):
        self.uid += 1
        return self.scopes[-1].enter_context(
            self.nc.sbuf_tensor("%s_%d" % (name, self.uid), list(shape), dt))

    def ps(self, name, shape, dt=F32):
        self.uid += 1
        return self.scopes[-1].enter_context(
            self.nc.psum_tensor("%s_%d" % (name, self.uid), list(shape), dt))

    def _dsem(self, tile, q):
        if tile.dsem is None:
            tile.dsem = {}
        if q not in tile.dsem:
            pool = self.free_dsems.setdefault(q, [])
            if pool:
                key = pool.pop()
            else:
                key = ("d", self.nsem)
                self.nsem += 1
                self.sems[key] = self.st.enter_context(self.nc.semaphore("d%d" % key[1]))
                self.total[key] = 0
            tile.dsem[q] = key
            self.phase_tiles.append((tile, q))
        return tile.dsem[q]

    def _collect(self, eng, reads, writes, acc):
        need = {}
        own = ("e", eng)
        for t in reads:
            for k, v in t.w.items():
                if need.get(k, 0) < v:
                    need[k] = v
            if t.excl:
                for k, v in t.r.items():
                    if k != own and need.get(k, 0) < v:
                        need[k] = v
        for t in writes:
            for d in (t.w, t.r):
                for k, v in d.items():
                    if acc and k == own:
                        continue
                    if need.get(k, 0) < v:
                        need[k] = v
        seen = self.seen[eng]
        waits = []
        for k, v in need.items():
            if seen.get(k, 0) < v:
                seen[k] = v
                waits.append((k, v))
        return waits

    @staticmethod
    def _finish(ev, reads, writes):
        k, v = ev
        for t in writes:
            t.w = {k: v}
            t.r = {}
        for t in reads:
            if t.r.get(k, 0) < v:
                t.r[k] = v

    def op(self, eng, fn, reads=(), writes=(), acc=False):
        waits = self._collect(eng, reads, writes, acc)
        key = ("e", eng)
        self.total[key] += 1
        ev = (key, self.total[key])
        self.streams[eng].append((waits, fn, (key, 1)))
        self._finish(ev, reads, writes)

    def dma(self, q, out, in_, reads=(), writes=(), st=None, **kw):
        waits = self._collect(q, reads, writes, False)
        key = self._dsem(st, q)
        self.total[key] += 16
        ev = (key, self.total[key])
        self.streams[q].append(
            (waits, lambda e: e.dma_start(out=out, in_=in_, **kw), (key, 16)))
        self._finish(ev, reads, writes)

    def barrier(self):
        waits = []
        for k, v in self.total.items():
            if self.bar_seen.get(k, 0) < v:
                self.bar_seen[k] = v
                waits.append((k, v))
        self.nbar += 1
        n = self.nbar
        bar = self.sems["bar"]
        self.streams["sync"].append((waits, lambda e: e.sem_inc(bar, 1), None))
        for e in self.ENGS:
            if e != "sync":
                self.streams[e].append(([("bar", n)], None, None))
            for k, v in self.total.items():
                self.seen[e][k] = v

    def emit(self):
        nc = self.nc
        sems = self.sems

        def run(eng_name):
            ops = self.streams[eng_name]

            def body(e):
                for waits, fn, inc in ops:
                    for k, v in waits:
                        e.wait_ge(sems[k], v)
                    if fn is not None:
                        ins = fn(e)
                        if inc is not None:
                            ins.then_inc(sems[inc[0]], inc[1])
            return body

        with nc.Block() as block:
            block.sync(run("sync"))
            block.scalar(run("scalar"))
            block.vector(run("vector"))
            block.gpsimd(run("gpsimd"))
            block.tensor(run("tensor"))
        self.streams = {e: [] for e in self.ENGS}

    def end_phase(self, name=""):
        if self.verbose:
            print("phase", name, "sbuf remaining", self.nc.sbuf_bytes_remaining,
                  {e: len(v) for e, v in self.streams.items()}, "nsem", self.nsem, flush=True)
        self.barrier()
        self.emit()
        for t, q in self.phase_tiles:
            self.free_dsems[q].append(t.dsem.pop(q))
        self.phase_tiles = []

    def close(self):
        self.st.close()


class Ring:
    def __init__(self, P, name, n, shape, dt, psum=False):
        self.bufs = []
        for i in range(n):
            t = P.ps(name, shape, dt) if psum else P.sb(name, shape, dt)
            self.bufs.append((t, Tile(name + str(i), excl=psum)))
        self.i = 0

    def next(self):
        b = self.bufs[self.i % len(self.bufs)]
        self.i += 1
        return b


def host_consts():
    cb = np.zeros((128, 3, 128), np.float32)
    cb[:, 0, :] = np.eye(128)
    cb[:, 1, :] = 1.0
    for dst in range(128):
        d = dst % 64
        if d < 8:
            cb[dst + 8, 2, dst] = -1.0
        elif d < 16:
            cb[dst - 8, 2, dst] = 1.0
    cf = np.zeros((128, 16), np.float32)
    inv = 500000.0 ** (-np.arange(0, 16, 2, dtype=np.float64) / 16.0)
    for r in range(128):
        d = r % 64
        cf[r, 0] = inv[d % 8] / TWO_PI if d < 16 else 0.0
    cf[:, 1] = np.arange(128)
    for a in range(4):
        for gl in range(2):
            lo = 32 * a + 16 * gl
            cf[lo:lo + 16, 2 + a * 2 + gl] = 1.0
    return cb.astype(ml_dtypes.bfloat16), cf


WNAMES = [
    ("norm_mix_g", [2, 1024]), ("w_in", [2, 1024, 7168]),
    ("lambda_q1", [2, 64]), ("lambda_k1", [2, 64]), ("lambda_q2", [2, 64]), ("lambda_k2", [2, 64]),
    ("attn_subln_g", [2, 128]), ("w_attn_br", [2, 1024, 1024]), ("w_fourier_br", [2, 512, 1024]),
    ("ssm_a_re", [2, 2, 32, 64]), ("ssm_a_im", [2, 2, 32, 64]), ("ssm_log_dt", [2, 2, 32]),
    ("ssm_b_re", [2, 32, 64, 16]), ("ssm_b_im", [2, 32, 64, 16]),
    ("ssm_c_re", [2, 32, 16, 64]), ("ssm_c_im", [2, 32, 16, 64]), ("ssm_d", [2, 512]),
    ("w_glu", [2, 512, 2048]), ("b_glu", [2, 2048]), ("w_out", [2, 1024, 1024]),
    ("norm_ffn_g", [2, 1024]), ("w_up", [2, 1024, 5632]), ("conv_w", [2, 3, 2816]),
    ("conv_b", [2, 2816]), ("w_down", [2, 2816, 1024]), ("norm_ple_g", [2, 1024]),
    ("w_ple_gate", [2, 1024, 1024]), ("w_ple", [2, 256, 1024]), ("final_norm_g", [1024]),
]


class Ctx:
    pass


def dump(P, C, name, sbt, shape, tiles, dt=BF16):
    d = P.nc.dram_tensor("dbg_" + name, list(shape), dt, kind="ExternalOutput").ap()
    T = Tile()
    P.dma("sync", d, sbt[:], reads=list(tiles), st=T)
    P.end_phase("dump")


def mm(P, out, lhsT, rhs, start, stop, reads, writes):
    P.op("tensor", lambda e: e.matmul(out, lhsT=lhsT, rhs=rhs, start=start, stop=stop),
         reads=reads, writes=writes, acc=not start)


def frac_turns(P, u, n, tag):
    ut, Tu = u
    ki = P.sb("ki" + tag, [128, n], I32)
    Tki = Tile()
    kf = P.sb("kf" + tag, [128, n], F32)
    Tkf = Tile()
    f = P.sb("fr" + tag, [128, n], F32)
    Tf = Tile()
    P.op("vector", lambda e: e.tensor_copy(out=ki[:], in_=ut[:]), reads=[Tu], writes=[Tki])
    P.op("gpsimd", lambda e: e.tensor_copy(out=kf[:], in_=ki[:]), reads=[Tki], writes=[Tkf])
    P.op("vector", lambda e: e.tensor_sub(out=f[:], in0=ut[:], in1=kf[:]), reads=[Tu, Tkf], writes=[Tf])
    return f, Tf


def sincos_from_frac(P, f, Tf, n, sin_out, Tsin, cos_out, Tcos, tag, sin_sign=1.0):
    g = P.sb("wg" + tag, [128, n], F32)
    Tg = Tile()
    f3 = P.sb("f3" + tag, [128, n], F32)
    Tf3 = Tile()
    P.op("scalar", lambda e: e.activation(out=sin_out, in_=f[:], func=AF.Sin, scale=SIN_SCALE * sin_sign),
         reads=[Tf], writes=[Tsin])
    P.op("vector", lambda e: e.tensor_scalar(out=g[:], in0=f[:], scalar1=0.25, scalar2=0.5,
                                             op0=ALU.add, op1=ALU.is_gt), reads=[Tf], writes=[Tg])
    P.op("vector", lambda e: e.scalar_tensor_tensor(out=f3[:], in0=f[:], scalar=0.25, in1=g[:],
                                                    op0=ALU.add, op1=ALU.subtract),
         reads=[Tf, Tg], writes=[Tf3])
    P.op("scalar", lambda e: e.activation(out=cos_out, in_=f3[:], func=AF.Sin, scale=SIN_SCALE),
         reads=[Tf3], writes=[Tcos])


def build(n_layers=2, n_seq=2, debug=False, stop=None):
    nc = bass.Bass("TRN2", target_bir_lowering=False)
    C = Ctx()
    import os
    C.norot = 'norot' in os.environ.get('KDBG', '')
    C.nosig = 'nosig' in os.environ.get('KDBG', '')

    def din(name, shape, dt=F32):
        return nc.dram_tensor(name, list(shape), dt, kind="ExternalInput").ap()

    C.x = din("x", [2, S, D])
    C.p = din("p", [2, 2, S, 256])
    C.pos = din("pos", [2, S], I32)
    C.cb = din("cb", [128, 3, 128], BF16)
    C.cf = din("cf", [128, 16])
    C.W = {n: din(n, s) for n, s in WNAMES}
    C.y = nc.dram_tensor("y", [2, S, D], F32, kind="ExternalOutput").ap()
    skind = "ExternalOutput" if debug else "Internal"

    def scratch(name, shape, dt):
        return nc.dram_tensor(name, list(shape), dt, kind=skind).ap()

    C.xres = scratch("xres", [2, S, D], F32)
    C.QT = scratch("QT", [8, 128, S], BF16)
    C.KT = scratch("KT", [8, 128, S], BF16)
    C.V = scratch("V", [S, 1024], BF16)
    C.ufT = scratch("ufT", [4, 128, S], BF16)
    C.usT = scratch("usT", [4, 128, S], BF16)
    C.gT = scratch("gT", [24, 128, S], BF16)
    C.dftc = scratch("dftc", [S, S], BF16)
    C.dfts = scratch("dfts", [S, S], BF16)
    C.dbg = {}

    P = Prog(nc)
    P.verbose = debug
    C.P = P
    C.cbs = P.sb("cbs", [128, 3, 128], BF16)
    C.Tcbs = Tile("cbs")
    C.cfs = P.sb("cfs", [128, 16], F32)
    C.Tcfs = Tile("cfs")
    P.dma("sync", C.cbs[:], C.cb, writes=[C.Tcbs], st=C.Tcbs)
    P.dma("sync", C.cfs[:], C.cf, writes=[C.Tcfs], st=C.Tcfs)
    C.ident = C.cbs[:, 0, :]
    C.ones = C.cbs[:, 1, :]
    C.Pm = C.cbs[:, 2, :]
    C.hT = P.sb("hT", [128, 8, S], BF16)
    C.ThT = [Tile("hT%d" % i) for i in range(4)]
    C.cc = P.sb("cc", [128, 256], BF16)
    C.Tcc = Tile("cc")
    C.idf = P.sb("idf", [128, 128], F32)
    C.Tidf = Tile("idf")
    P.op("vector", lambda e: e.tensor_copy(out=C.idf[:], in_=C.ident), reads=[C.Tcbs], writes=[C.Tidf])
    P.end_phase("consts")
    phase_tables(P, C)

    done = False
    for b in range(n_seq):
        for l in range(n_layers):
            xsrc = C.x[b] if l == 0 else C.xres[b]
            phase_norm(P, C, xsrc, C.W["norm_mix_g"][l])
            phase_win(P, C, l, b)
            if stop == "A":
                done = True
                break
            P.push()
            C.oT = P.sb("oT", [128, 8, S], BF16)
            C.ToT = [[Tile() for _ in range(4)] for _ in range(8)]
            C.fT = P.sb("fT", [128, 4, S], BF16)
            C.TfT = [[Tile() for _ in range(4)] for _ in range(4)]
            C.sT = P.sb("sT", [128, 4, S], BF16)
            C.TsT = [[Tile() for _ in range(4)] for _ in range(4)]
            if stop != "C":
                phase_attn(P, C, l)
            if stop == "B":
                dump(P, C, "oT", C.oT, [128, 8, S], [t for r in C.ToT for t in r])
                done = True
            if not done:
                phase_fourier(P, C)
            if stop == "C":
                dump(P, C, "fT", C.fT, [128, 4, S], [t for r in C.TfT for t in r])
                done = True
            if not done:
                phase_ssm(P, C, l)
            if stop == "D":
                dump(P, C, "sT", C.sT, [128, 4, S], [t for r in C.TsT for t in r])
                done = True
            if not done:
                phase_merge(P, C, l)
            P.pop()
            if done:
                break
            if stop == "E":
                dump(P, C, "mT", C.hT, [128, 8, S], C.ThT)
                done = True
                break
            phase_wout(P, C, l, b, xsrc)
            if stop == "F":
                done = True
                break
            phase_norm(P, C, C.xres[b], C.W["norm_ffn_g"][l])
            phase_ffn(P, C, l, b)
            if stop == "H":
                done = True
                break
            phase_norm(P, C, C.xres[b], C.W["norm_ple_g"][l])
            phase_ple(P, C, l, b)
        if done:
            break
        phase_final(P, C, b)
    P.close()
    return nc, C


def phase_norm(P, C, xsrc, gain):
    P.push()
    gb = P.sb("gb", [128, D], F32)
    Tgb = Tile()
    P.dma("sync", gb[:], gain.partition_broadcast(128), writes=[Tgb], st=Tgb)
    xr = Ring(P, "xt", 2, [128, D], F32)
    jr = Ring(P, "junk", 2, [128, D], BF16)
    sr = Ring(P, "ss", 2, [128, 2], F32)
    hr = Ring(P, "hb", 2, [128, D], BF16)
    pr = Ring(P, "pt", 2, [128, D], BF16, psum=True)
    for tt in range(NT):
        xt, Tx = xr.next()
        P.dma("sync", xt[:], xsrc[tt * 128:(tt + 1) * 128, :], writes=[Tx], st=Tx)
        norm_tile(P, C, xt, Tx, gb, Tgb, tt, jr, sr, hr, pr)
    P.end_phase()
    P.pop()


def norm_tile(P, C, xt, Tx, gb, Tgb, tt, jr, sr, hr, pr):
    jk, Tj = jr.next()
    ss, Tss = sr.next()
    hb, Thb = hr.next()
    pt, Tpt = pr.next()
    P.op("scalar", lambda e: e.activation(out=jk[:], in_=xt[:], func=AF.Square, accum_out=ss[:, 0:1]),
         reads=[Tx], writes=[Tj, Tss])
    P.op("vector", lambda e: e.tensor_scalar(out=ss[:, 1:2], in0=ss[:, 0:1], scalar1=1.0 / D, scalar2=EPS,
                                             op0=ALU.mult, op1=ALU.add), reads=[Tss], writes=[Tss])
    P.op("scalar", lambda e: e.activation(out=ss[:, 1:2], in_=ss[:, 1:2], func=AF.Sqrt), reads=[Tss], writes=[Tss])
    P.op("vector", lambda e: e.reciprocal(out=ss[:, 1:2], in_=ss[:, 1:2]), reads=[Tss], writes=[Tss])
    P.op("vector", lambda e: e.scalar_tensor_tensor(out=hb[:], in0=xt[:], scalar=ss[:, 1:2], in1=gb[:],
                                                    op0=ALU.mult, op1=ALU.mult),
         reads=[Tx, Tss, Tgb], writes=[Thb])
    for kt in range(8):
        P.op("tensor", (lambda kt: lambda e: e.transpose(pt[:, kt * 128:(kt + 1) * 128],
                                                          hb[:, kt * 128:(kt + 1) * 128], C.ident))(kt),
             reads=[Thb, C.Tcbs], writes=[Tpt], acc=(kt > 0))
    Th = C.ThT[tt // 4]
    P.op("scalar", lambda e: e.copy(out=C.hT[:, :, tt * 128:(tt + 1) * 128],
                                    in_=pt[:].rearrange("p (k t) -> p k t", k=8)),
         reads=[Tpt], writes=[Th], acc=True)


def phase_win(P, C, l, b):
    P.push()
    posi = P.sb("posi", [128, S], I32)
    Tposi = Tile()
    P.dma("sync", posi[:], C.pos[b].partition_broadcast(128), writes=[Tposi], st=Tposi)
    u = P.sb("ropeu", [128, S], F32)
    Tu = Tile()
    P.op("vector", lambda e: e.tensor_copy(out=u[:], in_=posi[:]), reads=[Tposi], writes=[Tu])
    P.op("vector", lambda e: e.tensor_scalar(out=u[:], in0=u[:], scalar1=C.cfs[:, 0:1], scalar2=None,
                                             op0=ALU.mult), reads=[Tu, C.Tcfs], writes=[Tu])
    f, Tf = frac_turns(P, (u, Tu), S, "rope")
    crot = P.sb("crot", [128, S], F32)
    srot = P.sb("srot", [128, S], F32)
    Tcr, Tsr = Tile(), Tile()
    sincos_from_frac(P, f, Tf, S, srot[:], Tsr, crot[:], Tcr, "rope")

    W = C.W["w_in"][l]
    wst = Ring(P, "wst", 2, [128, 8, 512], F32)
    wbf = Ring(P, "wbf", 2, [128, 8, 512], BF16)
    acc = Ring(P, "acc", 3, [128, 512], F32, psum=True)
    pq = Ring(P, "pq", 2, [128, 512], F32, psum=True)
    qbf = Ring(P, "qbf", 2, [128, 512], BF16)
    t1r = Ring(P, "t1", 2, [128, 512], F32)
    t2r = Ring(P, "t2", 2, [128, 512], F32)
    obr = Ring(P, "ob", 3, [128, S], BF16)
    vbr = Ring(P, "vb", 2, [128, 512], BF16)

    pending = []

    def flush():
        while pending:
            pending.pop(0)()

    for slab in range(14):
        c0 = slab * 512
        ws, Tws = wst.next()
        wb, Twb = wbf.next()
        P.dma("sync", ws[:], W[:, c0:c0 + 512].rearrange("(kt p) c -> p kt c", p=128), writes=[Tws], st=Tws)
        P.op("gpsimd", lambda e, ws=ws, wb=wb: e.tensor_copy(out=wb[:], in_=ws[:]), reads=[Tws], writes=[Twb])
        if 4 <= slab < 6:
            for tt in range(NT):
                a, Ta = acc.next()
                for kt in range(8):
                    mm(P, a[:], C.hT[:, kt, tt * 128:(tt + 1) * 128], wb[:, kt, :], kt == 0, kt == 7,
                       [C.ThT[tt // 4], Twb], [Ta])
                flush()
                vb, Tvb = vbr.next()
                P.op("scalar", lambda e, vb=vb, a=a: e.copy(out=vb[:], in_=a[:]), reads=[Ta], writes=[Tvb])
                P.dma("gpsimd", C.V[tt * 128:(tt + 1) * 128, c0 - 2048:c0 - 2048 + 512], vb[:],
                      reads=[Tvb], st=Tvb)
            continue
        for dti in range(4):
            col = c0 + dti * 128
            ob, Tob = obr.next()
            for blk in range(4):
                a, Ta = acc.next()
                tb = slice(blk * 512, (blk + 1) * 512)
                for kt in range(8):
                    mm(P, a[:], wb[:, kt, dti * 128:(dti + 1) * 128], C.hT[:, kt, tb], kt == 0, kt == 7,
                       [C.ThT[blk], Twb], [Ta])
                flush()
                if col < 2048 and not C.norot:
                    qb, Tqb = qbf.next()
                    p2, Tp2 = pq.next()
                    t1, Tt1 = t1r.next()
                    t2, Tt2 = t2r.next()
                    P.op("scalar", lambda e, qb=qb, a=a: e.copy(out=qb[:], in_=a[:]), reads=[Ta], writes=[Tqb])
                    P.op("vector", lambda e, t1=t1, a=a, tb=tb: e.tensor_mul(out=t1[:], in0=a[:], in1=crot[:, tb]),
                         reads=[Ta, Tcr], writes=[Tt1])

                    def later(qb=qb, Tqb=Tqb, p2=p2, Tp2=Tp2, t1=t1, Tt1=Tt1, t2=t2, Tt2=Tt2, ob=ob, Tob=Tob, tb=tb):
                        mm(P, p2[:], C.Pm, qb[:], True, True, [Tqb, C.Tcbs], [Tp2])
                        P.op("vector", lambda e: e.tensor_mul(out=t2[:], in0=p2[:], in1=srot[:, tb]),
                             reads=[Tp2, Tsr], writes=[Tt2])
                        P.op("gpsimd", lambda e: e.tensor_add(out=ob[:, tb], in0=t1[:], in1=t2[:]),
                             reads=[Tt1, Tt2], writes=[Tob], acc=True)
                    pending.append(later)
                elif col < 4096 or C.nosig:
                    P.op("scalar", lambda e, ob=ob, a=a, tb=tb: e.copy(out=ob[:, tb], in_=a[:]),
                         reads=[Ta], writes=[Tob], acc=True)
                else:
                    P.op("scalar", lambda e, ob=ob, a=a, tb=tb: e.activation(out=ob[:, tb], in_=a[:], func=AF.Sigmoid),
                         reads=[Ta], writes=[Tob], acc=True)
            flush()
            if col < 1024:
                dst = C.QT[col // 128]
            elif col < 2048:
                dst = C.KT[(col - 1024) // 128]
            elif col < 3584:
                dst = C.ufT[(col - 3072) // 128]
            elif col < 4096:
                dst = C.usT[(col - 3584) // 128]
            else:
                dst = C.gT[(col - 4096) // 128]
            P.dma("gpsimd", dst, ob[:], reads=[Tob], st=Tob)
    P.end_phase()
    P.pop()


def load_cast(P, dst_bf, Tdst, src_ap, st_ring, eng="gpsimd"):
    ws, Tws = st_ring.next()
    shp = list(src_ap.shape)
    view = ws[:, 0:shp[1], 0:shp[2]]
    P.dma("sync", view, src_ap, writes=[Tws], st=Tws)
    P.op(eng, lambda e: e.tensor_copy(out=dst_bf, in_=view), reads=[Tws], writes=[Tdst], acc=True)


def load_cols(P, C, dst, Tdst, rows_ap, n, ps_ring, name):
    raw = P.sb("raw_" + name, [n, 128], F32)
    Traw = Tile()
    P.dma("sync", raw[:], rows_ap, writes=[Traw], st=Traw)
    ps, Tps = ps_ring.next()
    P.op("tensor", lambda e: e.transpose(ps[:, 0:n], raw[0:n, :], C.idf[0:n, 0:n]), reads=[Traw, C.Tidf], writes=[Tps])
    P.op("vector", lambda e: e.tensor_copy(out=dst, in_=ps[:, 0:n]), reads=[Tps], writes=[Tdst])


def phase_attn(P, C, l):
    lam_init = 0.8 - 0.6 * math.exp(-0.3 * l)
    P.push()
    lq = P.sb("lq", [128, 4, 64], F32)
    Tlq = Tile()
    for i, n in enumerate(["lambda_q1", "lambda_k1", "lambda_q2", "lambda_k2"]):
        P.dma("sync", lq[:, i, :], C.W[n][l].partition_broadcast(128), writes=[Tlq], st=Tlq)
    sc = P.sb("lsc", [128, 8], F32)
    Tsc = Tile()
    pr = P.sb("lpr", [128, 2, 64], F32)
    Tpr = Tile()
    P.op("vector", lambda e: e.tensor_mul(out=pr[:, 0, :], in0=lq[:, 0, :], in1=lq[:, 1, :]), reads=[Tlq], writes=[Tpr])
    P.op("vector", lambda e: e.tensor_mul(out=pr[:, 1, :], in0=lq[:, 2, :], in1=lq[:, 3, :]), reads=[Tlq], writes=[Tpr])
    P.op("vector", lambda e: e.reduce_sum(out=sc[:, 0:2], in_=pr[:], axis=mybir.AxisListType.X), reads=[Tpr], writes=[Tsc])
    P.op("scalar", lambda e: e.activation(out=sc[:, 2:4], in_=sc[:, 0:2], func=AF.Exp), reads=[Tsc], writes=[Tsc])
    P.op("vector", lambda e: e.tensor_sub(out=sc[:, 4:5], in0=sc[:, 3:4], in1=sc[:, 2:3]), reads=[Tsc], writes=[Tsc])
    P.op("vector", lambda e: e.tensor_scalar(out=sc[:, 4:5], in0=sc[:, 4:5], scalar1=-lam_init, scalar2=None, op0=ALU.add),
         reads=[Tsc], writes=[Tsc])
    P.dma("sync", sc[:, 5:6], C.W["attn_subln_g"][l].rearrange("(a b) -> a b", b=1), writes=[Tsc], st=Tsc)
    P.op("vector", lambda e: e.tensor_scalar(out=sc[:, 5:6], in0=sc[:, 5:6], scalar1=1.0 - lam_init, scalar2=None, op0=ALU.mult),
         reads=[Tsc], writes=[Tsc])
    neg_lam = sc[:, 4:5]
    gcol = sc[:, 5:6]

    qr = Ring(P, "aq", 2, [128, S], BF16)
    kr = Ring(P, "ak", 2, [128, S], BF16)
    vr = Ring(P, "av", 2, [128, NT, 128], BF16)
    sps = Ring(P, "sps", 2, [128, 512], F32, psum=True)
    ops_ = [Ring(P, "ops", 1, [128, 512], F32, psum=True) for _ in range(2)]
    zps = [Ring(P, "zps", 1, [128, 512], F32, psum=True) for _ in range(2)]
    ssp = Ring(P, "ssp", 1, [128, 512], F32, psum=True)
    er = Ring(P, "ae", 4, [128, 512], BF16)
    r0r = Ring(P, "ar0", 2, [128, 512], F32)
    t0r = Ring(P, "at0", 2, [128, 512], F32)
    t1r = Ring(P, "at1", 2, [128, 512], F32)
    orr = Ring(P, "ao", 2, [128, 512], F32)
    sqr = Ring(P, "asq", 2, [128, 512], BF16)
    rsr = Ring(P, "ars", 2, [128, 512], F32)
    for h in range(8):
        q, Tq = qr.next()
        k, Tk = kr.next()
        v, Tv = vr.next()
        P.dma("sync", q[:], C.QT[h], writes=[Tq], st=Tq)
        P.dma("sync", k[:], C.KT[h], writes=[Tk], st=Tk)
        P.dma("sync", v[:], C.V[:, h * 128:(h + 1) * 128].rearrange("(tt p) e -> p tt e", p=128), writes=[Tv], st=Tv)
        for qb in range(4):
            qs = slice(qb * 512, (qb + 1) * 512)
            accs = []
            for c in range(2):
                o, To = ops_[c].next()
                zz, Tz = zps[c].next()
                accs.append((o, To, zz, Tz))
                prev = None
                for kt in range(NT):
                    sp, Tsp = sps.next()
                    mm(P, sp[:], k[c * 64:(c + 1) * 64, kt * 128:(kt + 1) * 128], q[c * 64:(c + 1) * 64, qs],
                       True, True, [Tq, Tk], [Tsp])
                    if prev is not None:
                        pe_, Tpe, pkt = prev
                        mm(P, o[:], v[:, pkt, :], pe_[:], pkt == 0, False, [Tv, Tpe], [To])
                        mm(P, zz[:], C.ones, pe_[:], pkt == 0, False, [C.Tcbs, Tpe], [Tz])
                    e_, Te = er.next()
                    P.op("scalar", lambda e, e_=e_, sp=sp: e.activation(out=e_[:], in_=sp[:], func=AF.Exp, scale=0.125),
                         reads=[Tsp], writes=[Te])
                    prev = (e_, Te, kt)
                pe_, Tpe, pkt = prev
                mm(P, o[:], v[:, pkt, :], pe_[:], False, True, [Tv, Tpe], [To])
                mm(P, zz[:], C.ones, pe_[:], False, True, [C.Tcbs, Tpe], [Tz])
            (o0, To0, z0, Tz0), (o1, To1, z1, Tz1) = accs
            r0, Tr0 = r0r.next()
            t0, Tt0 = t0r.next()
            t1, Tt1 = t1r.next()
            oo, Too = orr.next()
            sq, Tsq = sqr.next()
            rs, Trs = rsr.next()
            ss, Tss = ssp.next()
            P.op("vector", lambda e, r0=r0, z0=z0: e.reciprocal(out=r0[:], in_=z0[:]), reads=[Tz0], writes=[Tr0])
            P.op("vector", lambda e, t0=t0, o0=o0, r0=r0: e.tensor_mul(out=t0[:], in0=o0[:], in1=r0[:]), reads=[To0, Tr0], writes=[Tt0])
            P.op("vector", lambda e, r0=r0, z1=z1: e.reciprocal(out=r0[:], in_=z1[:]), reads=[Tz1, Tt0], writes=[Tr0])
            P.op("vector", lambda e, t1=t1, o1=o1, r0=r0: e.tensor_mul(out=t1[:], in0=o1[:], in1=r0[:]), reads=[To1, Tr0], writes=[Tt1])
            P.op("vector", lambda e, oo=oo, t1=t1, t0=t0: e.scalar_tensor_tensor(out=oo[:], in0=t1[:], scalar=neg_lam, in1=t0[:],
                                                                             op0=ALU.mult, op1=ALU.add),
                 reads=[Tt1, Tt0, Tsc], writes=[Too])
            P.op("gpsimd", lambda e, sq=sq, oo=oo: e.tensor_mul(out=sq[:], in0=oo[:], in1=oo[:]), reads=[Too], writes=[Tsq])
            mm(P, ss[:], C.ones, sq[:], True, True, [C.Tcbs, Tsq], [Tss])
            P.op("vector", lambda e, rs=rs, ss=ss: e.tensor_scalar(out=rs[:], in0=ss[:], scalar1=1.0 / 128, scalar2=EPS,
                                                                   op0=ALU.mult, op1=ALU.add), reads=[Tss], writes=[Trs])
            P.op("scalar", lambda e, rs=rs: e.activation(out=rs[:], in_=rs[:], func=AF.Sqrt), reads=[Trs], writes=[Trs])
            P.op("vector", lambda e, rs=rs: e.reciprocal(out=rs[:], in_=rs[:]), reads=[Trs], writes=[Trs])
            P.op("vector", lambda e, oo=oo, rs=rs, h=h, qs=qs: e.scalar_tensor_tensor(out=C.oT[:, h, qs], in0=oo[:], scalar=gcol, in1=rs[:],
                                                                                   op0=ALU.mult, op1=ALU.mult),
                 reads=[Too, Trs, Tsc], writes=[C.ToT[h][qb]])
    P.end_phase("attn")
    P.pop()


def phase_tables(P, C):
    P.push()
    io = P.sb("io", [128, S], F32)
    Tio = Tile()
    P.op("gpsimd", lambda e: e.iota(io[:], [[1, S]], base=0, channel_multiplier=0, allow_small_or_imprecise_dtypes=True),
         writes=[Tio])
    ur = Ring(P, "tu", 2, [128, S], F32)
    kir = Ring(P, "tki", 2, [128, S], I32)
    kfr = Ring(P, "tkf", 2, [128, S], F32)
    fr = Ring(P, "tf", 2, [128, S], F32)
    gr = Ring(P, "tg", 2, [128, S], F32)
    f3r = Ring(P, "tf3", 2, [128, S], F32)
    sr = Ring(P, "tsin", 2, [128, S], BF16)
    cr = Ring(P, "tcos", 2, [128, S], BF16)
    scol = P.sb("scol", [128, 17], F32)
    Tscol = Tile()
    for st in range(17):
        P.op("vector", lambda e, st=st: e.tensor_scalar(out=scol[:, st:st + 1], in0=C.cfs[:, 1:2], scalar1=float(128 * st if st < 16 else 0),
                                                        scalar2=None, op0=ALU.add), reads=[C.Tcfs], writes=[Tscol], acc=True)
    for st in range(17):
        n = S if st < 16 else 128
        inv = 1.0 / (S if st < 16 else 128)
        u, Tu = ur.next()
        ki, Tki = kir.next()
        kf, Tkf = kfr.next()
        f, Tf = fr.next()
        g, Tg = gr.next()
        f3, Tf3 = f3r.next()
        sn, Tsn = sr.next()
        cs, Tcs = cr.next()
        P.op("vector", lambda e, u=u, st=st, n=n, inv=inv: e.tensor_scalar(out=u[:, :n], in0=io[:, :n], scalar1=scol[:, st:st + 1], scalar2=inv,
                                                                           op0=ALU.mult, op1=ALU.mult), reads=[Tio, Tscol], writes=[Tu])
        P.op("vector", lambda e, u=u, ki=ki, n=n: e.tensor_copy(out=ki[:, :n], in_=u[:, :n]), reads=[Tu], writes=[Tki])
        P.op("gpsimd", lambda e, kf=kf, ki=ki, n=n: e.tensor_copy(out=kf[:, :n], in_=ki[:, :n]), reads=[Tki], writes=[Tkf])
        P.op("gpsimd", lambda e, f=f, u=u, kf=kf, n=n: e.tensor_sub(out=f[:, :n], in0=u[:, :n], in1=kf[:, :n]), reads=[Tu, Tkf], writes=[Tf])
        sgn = 1.0 if st < 16 else -1.0
        P.op("scalar", lambda e, sn=sn, f=f, n=n, sgn=sgn: e.activation(out=sn[:, :n], in_=f[:, :n], func=AF.Sin, scale=SIN_SCALE * sgn),
             reads=[Tf], writes=[Tsn])
        P.op("vector", lambda e, g=g, f=f, n=n: e.tensor_scalar(out=g[:, :n], in0=f[:, :n], scalar1=0.25, scalar2=0.5, op0=ALU.add, op1=ALU.is_gt),
             reads=[Tf], writes=[Tg])
        P.op("vector", lambda e, f3=f3, f=f, g=g, n=n: e.scalar_tensor_tensor(out=f3[:, :n], in0=f[:, :n], scalar=0.25, in1=g[:, :n],
                                                                            op0=ALU.add, op1=ALU.subtract), reads=[Tf, Tg], writes=[Tf3])
        P.op("scalar", lambda e, cs=cs, f3=f3, n=n: e.activation(out=cs[:, :n], in_=f3[:, :n], func=AF.Sin, scale=SIN_SCALE),
             reads=[Tf3], writes=[Tcs])
        if st < 16:
            P.dma("sync", C.dftc[st * 128:(st + 1) * 128, :], cs[:], reads=[Tcs], st=Tcs)
            P.dma("sync", C.dfts[st * 128:(st + 1) * 128, :], sn[:], reads=[Tsn], st=Tsn)
        else:
            P.op("gpsimd", lambda e, cs=cs: e.tensor_copy(out=C.cc[:, 0:128], in_=cs[:, 0:128]), reads=[Tcs], writes=[C.Tcc])
            P.op("gpsimd", lambda e, sn=sn: e.tensor_copy(out=C.cc[:, 128:256], in_=sn[:, 0:128]), reads=[Tsn], writes=[C.Tcc])
    P.end_phase("tables")
    P.pop()


def phase_fourier(P, C):
    P.push()
    uf = P.sb("uf", [128, 4, S], BF16)
    Tuf = Tile()
    P.dma("sync", uf[:], C.ufT.rearrange("g p s -> p g s"), writes=[Tuf], st=Tuf)
    G = P.sb("G", [128, NT, 4, 256], BF16)
    TG = Tile()
    gps = Ring(P, "gps", 3, [128, 512], F32, psum=True)
    fps = Ring(P, "fps", 3, [128, 512], F32, psum=True)
    n = 0
    for tt in range(NT):
        for g in range(4):
            gp, Tgp = gps.next()
            mm(P, gp[:, 0:256], uf[:, g, tt * 128:(tt + 1) * 128], C.cc[:, :], True, True, [Tuf, C.Tcc], [Tgp])
            eng = "scalar" if n % 2 == 0 else "vector"
            n += 1
            if eng == "scalar":
                P.op("scalar", lambda e, gp=gp, tt=tt, g=g: e.copy(out=G[:, tt, g, :], in_=gp[:, 0:256]), reads=[Tgp], writes=[TG], acc=True)
            else:
                P.op("vector", lambda e, gp=gp, tt=tt, g=g: e.tensor_copy(out=G[:, tt, g, :], in_=gp[:, 0:256]), reads=[Tgp], writes=[TG], acc=True)
    tcr = Ring(P, "tabc", 1, [128, NT, 512], BF16)
    tsr = Ring(P, "tabs", 1, [128, NT, 512], BF16)
    for j in range(4):
        tc, Ttc = tcr.next()
        ts, Tts = tsr.next()
        P.dma("sync", tc[:], C.dftc[:, j * 512:(j + 1) * 512].rearrange("(st p) c -> p st c", p=128), writes=[Ttc], st=Ttc)
        P.dma("sync", ts[:], C.dfts[:, j * 512:(j + 1) * 512].rearrange("(st p) c -> p st c", p=128), writes=[Tts], st=Tts)
        for g in range(4):
            fp, Tfp = fps.next()
            for st in range(NT):
                mm(P, fp[:], G[:, st, g, 0:128], tc[:, st, :], st == 0, False, [TG, Ttc], [Tfp])
                mm(P, fp[:], G[:, st, g, 128:256], ts[:, st, :], False, st == NT - 1, [TG, Tts], [Tfp])
            P.op("scalar", lambda e, fp=fp, g=g, j=j: e.activation(out=C.fT[:, g, j * 512:(j + 1) * 512], in_=fp[:], func=AF.Copy, scale=1.0 / 512),
                 reads=[Tfp], writes=[C.TfT[g][j]])
    P.end_phase("fourier")
    P.pop()


_CACHE = {}


def kernel(**inputs):
    n = 8
    if "nc" not in _CACHE:
        _CACHE["nc"] = build()[0]
    nc = _CACHE["nc"]
    cb, cf = host_consts()
    in_maps = []
    for c in range(n):
        m = {"x": np.ascontiguousarray(inputs["x"][2 * c:2 * c + 2]),
             "p": np.ascontiguousarray(inputs["p"][:, 2 * c:2 * c + 2]),
             "pos": np.ascontiguousarray(inputs["positions"][2 * c:2 * c + 2]).astype(np.int32),
             "cb": cb, "cf": cf}
        for name, _ in WNAMES:
            m[name] = np.ascontiguousarray(inputs[name])
        in_maps.append(m)
    res = run_bass_kernel_spmd(nc, in_maps, core_ids=list(range(n)))
    return np.concatenate([np.asarray(r["y"]) for r in res.results], axis=0).astype(np.float32)


def phase_ssm(P, C, l):
    TC = 128
    NCH = S // TC
    W = C.W
    P.push()
    ctab = P.sb("ctab", [128, 32, TC + 1], F32)
    stab = P.sb("stab", [128, 32, TC + 1], F32)
    Ttab = Tile()
    rho = P.sb("rho", [128, 32], F32)
    Trho = Tile()
    Bl = P.sb("Bl", [128, 16, 2, 128], BF16)
    TBl = Tile()
    Cl = P.sb("Cl", [128, 2, 16, 2, 128], BF16)
    TCl = Tile()
    dcol = P.sb("dcol", [128, 4], F32)
    Tdcol = Tile()
    P.push()
    idf, Tidf = C.idf, C.Tidf
    araw = P.sb("araw", [32, 2, 128], F32)
    Taraw = Tile()
    P.dma("sync", araw[:, 0, :], W["ssm_a_re"][l].rearrange("d g p -> (d g p)").rearrange("(r q) -> r q", q=128), writes=[Taraw], st=Taraw)
    P.dma("sync", araw[:, 1, :], W["ssm_a_im"][l].rearrange("d g p -> (d g p)").rearrange("(r q) -> r q", q=128), writes=[Taraw], st=Taraw)
    pp = Ring(P, "spp", 2, [128, 512], F32, psum=True)
    load_cols(P, C, dcol[:], Tdcol, W["ssm_d"][l].rearrange("(t p) -> t p", p=128), 4, pp, "dcol")
    pb = Ring(P, "spb", 2, [128, 1024], BF16, psum=True)
    ps0, Tps0 = pp.next()
    for ri in range(2):
        P.op("tensor", lambda e, ri=ri: e.transpose(ps0[:, ri * 32:(ri + 1) * 32], araw[0:32, ri, :], idf[0:32, 0:32]),
             reads=[Taraw, Tidf], writes=[Tps0], acc=(ri > 0))
    atr = P.sb("atr", [128, 2, 32], F32)
    Tatr = Tile()
    P.op("vector", lambda e: e.tensor_copy(out=atr[:], in_=ps0[:, 0:64].rearrange("p (a b) -> p a b", a=2)), reads=[Tps0], writes=[Tatr])
    ldt = P.sb("ldt", [128, 64], F32)
    Tldt = Tile()
    P.dma("sync", ldt[:], W["ssm_log_dt"][l].rearrange("d g -> (d g)").partition_broadcast(128), writes=[Tldt], st=Tldt)
    dtm = P.sb("dtm", [128, 32], F32)
    Tdtm = Tile()
    for gl in range(2):
        P.op("vector", lambda e, gl=gl: e.tensor_copy(
            out=dtm[gl * 64:(gl + 1) * 64, :].rearrange("p (d j) -> p d j", d=2),
            in_=ldt[gl * 64:(gl + 1) * 64, :].rearrange("p (d j g) -> p d j g", d=2, g=2)[:, :, :, gl]),
            reads=[Tldt], writes=[Tdtm], acc=(gl > 0))
    P.op("scalar", lambda e: e.activation(out=dtm[:], in_=dtm[:], func=AF.Exp), reads=[Tdtm], writes=[Tdtm])
    sm = P.sb("sm", [128, 16, 32], F32)
    Tsm = Tile()
    th = sm[:, 0, :]
    ar = sm[:, 1, :]
    P.op("vector", lambda e: e.tensor_mul(out=ar, in0=atr[:, 0, :], in1=dtm[:]), reads=[Tatr, Tdtm], writes=[Tsm])
    P.op("vector", lambda e: e.scalar_tensor_tensor(out=th, in0=atr[:, 1, :], scalar=1.0 / TWO_PI, in1=dtm[:], op0=ALU.mult, op1=ALU.mult),
         reads=[Tatr, Tdtm], writes=[Tsm])
    P.op("scalar", lambda e: e.activation(out=rho[:], in_=ar, func=AF.Exp), reads=[Tsm], writes=[Trho])
    tio = P.sb("tio", [128, TC + 1], F32)
    Ttio = Tile()
    P.op("gpsimd", lambda e: e.iota(tio[:], [[1, TC + 1]], base=0, channel_multiplier=0, allow_small_or_imprecise_dtypes=True), writes=[Ttio])
    NW = 8 * (TC + 1)
    u = P.sb("su", [128, 8, TC + 1], F32)
    ki = P.sb("ski", [128, NW], I32)
    kf = P.sb("skf", [128, NW], F32)
    f = P.sb("sf", [128, NW], F32)
    g = P.sb("sg", [128, NW], F32)
    f3 = P.sb("sf3", [128, NW], F32)
    Tu, Tki, Tkf, Tf, Tg, Tf3 = (Tile() for _ in range(6))
    for d8 in range(4):
        for j in range(8):
            dj = d8 * 8 + j
            P.op("vector", lambda e, j=j, dj=dj: e.tensor_scalar(out=u[:, j, :], in0=tio[:], scalar1=sm[:, 0, dj:dj + 1], scalar2=None, op0=ALU.mult),
                 reads=[Ttio, Tsm], writes=[Tu], acc=(j > 0))
        uf_ = u[:].rearrange("p a b -> p (a b)")
        P.op("vector", lambda e: e.tensor_copy(out=ki[:], in_=uf_), reads=[Tu], writes=[Tki])
        P.op("gpsimd", lambda e: e.tensor_copy(out=kf[:], in_=ki[:]), reads=[Tki], writes=[Tkf])
        P.op("gpsimd", lambda e: e.tensor_sub(out=f[:], in0=uf_, in1=kf[:]), reads=[Tu, Tkf], writes=[Tf])
        so = stab[:, d8 * 8:(d8 + 1) * 8, :].rearrange("p a b -> p (a b)")
        co = ctab[:, d8 * 8:(d8 + 1) * 8, :].rearrange("p a b -> p (a b)")
        P.op("scalar", lambda e, so=so: e.activation(out=so, in_=f[:], func=AF.Sin, scale=SIN_SCALE), reads=[Tf], writes=[Ttab], acc=True)
        P.op("vector", lambda e: e.tensor_scalar(out=g[:], in0=f[:], scalar1=0.25, scalar2=0.5, op0=ALU.add, op1=ALU.is_gt), reads=[Tf], writes=[Tg])
        P.op("vector", lambda e: e.scalar_tensor_tensor(out=f3[:], in0=f[:], scalar=0.25, in1=g[:], op0=ALU.add, op1=ALU.subtract),
             reads=[Tf, Tg], writes=[Tf3])
        P.op("scalar", lambda e, co=co: e.activation(out=co, in_=f3[:], func=AF.Sin, scale=SIN_SCALE), reads=[Tf3], writes=[Ttab], acc=True)
    c1 = ctab[:, :, 1]
    s1 = stab[:, :, 1]
    nr, ni, m2, t1, t2 = sm[:, 2, :], sm[:, 3, :], sm[:, 4, :], sm[:, 5, :], sm[:, 6, :]
    bsre, bsim, nbsim = sm[:, 8, :], sm[:, 9, :], sm[:, 10, :]
    lr, li = atr[:, 0, :], atr[:, 1, :]
    V = lambda fn, rd=(), wr=(Tsm,): P.op("vector", fn, reads=[Tsm, Ttab, Trho, Tatr] + list(rd), writes=list(wr))
    V(lambda e: e.tensor_mul(out=nr, in0=rho[:], in1=c1))
    V(lambda e: e.tensor_scalar(out=nr, in0=nr, scalar1=-1.0, scalar2=None, op0=ALU.add))
    V(lambda e: e.tensor_mul(out=ni, in0=rho[:], in1=s1))
    V(lambda e: e.tensor_mul(out=m2, in0=lr, in1=lr))
    V(lambda e: e.tensor_mul(out=t1, in0=li, in1=li))
    V(lambda e: e.tensor_add(out=m2, in0=m2, in1=t1))
    V(lambda e: e.reciprocal(out=m2, in_=m2))
    V(lambda e: e.tensor_mul(out=t1, in0=nr, in1=lr))
    V(lambda e: e.tensor_mul(out=t2, in0=ni, in1=li))
    V(lambda e: e.tensor_add(out=t1, in0=t1, in1=t2))
    V(lambda e: e.tensor_mul(out=bsre, in0=t1, in1=m2))
    V(lambda e: e.tensor_mul(out=t1, in0=ni, in1=lr))
    V(lambda e: e.tensor_mul(out=t2, in0=nr, in1=li))
    V(lambda e: e.tensor_sub(out=t1, in0=t1, in1=t2))
    V(lambda e: e.tensor_mul(out=bsim, in0=t1, in1=m2))
    V(lambda e: e.tensor_scalar(out=nbsim, in0=bsim, scalar1=-1.0, scalar2=None, op0=ALU.mult))
    Bnat = P.sb("Bnat", [128, 2, 16, 16], F32)
    TBn = Tile()
    P.dma("sync", Bnat[:, 0, :, :], W["ssm_b_re"][l].rearrange("g p h -> (g p h)").rearrange("(j q h) -> q j h", j=16, q=128), writes=[TBn], st=TBn)
    P.dma("sync", Bnat[:, 1, :, :], W["ssm_b_im"][l].rearrange("g p h -> (g p h)").rearrange("(j q h) -> q j h", j=16, q=128), writes=[TBn], st=TBn)
    XB = P.sb("XB", [128, 16, 2, 128], BF16)
    TXB = Tile()
    P.op("gpsimd", lambda e: e.memset(XB[:], 0.0), writes=[TXB])
    for j in range(16):
        for ri in range(2):
            for gl in range(2):
                c0 = 32 * (j % 4) + 16 * gl
                P.op("vector",
                     lambda e, j=j, ri=ri, gl=gl, c0=c0: e.tensor_copy(out=XB[gl * 64:(gl + 1) * 64, j, ri, c0:c0 + 16],
                                                                      in_=Bnat[gl * 64:(gl + 1) * 64, ri, j, :]),
                     reads=[TBn], writes=[TXB], acc=True)
    for grp in range(4):
        pbt, Tpbt = pb.next()
        for i in range(8):
            idx = grp * 8 + i
            j, ri = idx // 2, idx % 2
            P.op("tensor", lambda e, i=i, j=j, ri=ri, pbt=pbt: e.transpose(pbt[:, i * 128:(i + 1) * 128], XB[:, j, ri, :], C.ident),
                 reads=[TXB, C.Tcbs], writes=[Tpbt], acc=(i > 0))
        P.op("scalar", lambda e, grp=grp, pbt=pbt: e.copy(out=Bl[:, grp * 4:(grp + 1) * 4, :, :].rearrange("p a b c -> p (a b c)"), in_=pbt[:]),
             reads=[Tpbt], writes=[TBl], acc=True)
    Craw = P.sb("Craw", [128, 2, 4, 64], F32)
    TCr = Tile()
    P.dma("sync", Craw[:, 0, :, :], W["ssm_c_re"][l].rearrange("g h p -> (g h p)").rearrange("(t m p) -> m t p", t=4, m=128), writes=[TCr], st=TCr)
    P.dma("sync", Craw[:, 1, :, :], W["ssm_c_im"][l].rearrange("g h p -> (g h p)").rearrange("(t m p) -> m t p", t=4, m=128), writes=[TCr], st=TCr)
    xcr = Ring(P, "XC", 2, [128, 2, 128], F32)
    tmpr = Ring(P, "ctmp", 2, [128, 128], F32)
    n = 0
    for j in range(16):
        XC, TXC = xcr.next()
        for ri in range(2):
            for gl in range(2):
                mc = 2 + (j % 4) * 2 + gl
                P.op("vector" if n % 2 == 0 else "gpsimd",
                     lambda e, XC=XC, j=j, ri=ri, gl=gl, mc=mc: e.tensor_scalar(out=XC[:, ri, gl * 64:(gl + 1) * 64], in0=Craw[:, ri, j // 4, :],
                                                                               scalar1=C.cfs[:, mc:mc + 1], scalar2=None, op0=ALU.mult),
                     reads=[TCr, C.Tcfs], writes=[TXC], acc=False)
                n += 1
        pt, Tpt = pp.next()
        for ri in range(2):
            P.op("tensor", lambda e, XC=XC, ri=ri, pt=pt: e.transpose(pt[:, ri * 128:(ri + 1) * 128], XC[:, ri, :], idf[:]),
                 reads=[TXC, Tidf], writes=[Tpt], acc=(ri > 0))
        tre, tim = pt[:, 0:128], pt[:, 128:256]
        for d in range(2):
            dj = d * 16 + j
            tm, Ttm = tmpr.next()
            P.op("vector", lambda e, tm=tm, tim=tim, dj=dj: e.tensor_scalar(out=tm[:], in0=tim, scalar1=sm[:, 9, dj:dj + 1], scalar2=None, op0=ALU.mult),
                 reads=[Tpt, Tsm], writes=[Ttm])
            P.op("vector", lambda e, tm=tm, tre=tre, d=d, j=j, dj=dj: e.scalar_tensor_tensor(out=Cl[:, d, j, 0, :], in0=tre, scalar=sm[:, 8, dj:dj + 1], in1=tm[:],
                                                                                          op0=ALU.mult, op1=ALU.subtract),
                 reads=[Tpt, Tsm, Ttm], writes=[TCl], acc=True)
            tm2, Ttm2 = tmpr.next()
            P.op("vector", lambda e, tm2=tm2, tim=tim, dj=dj: e.tensor_scalar(out=tm2[:], in0=tim, scalar1=sm[:, 8, dj:dj + 1], scalar2=None, op0=ALU.mult),
                 reads=[Tpt, Tsm], writes=[Ttm2])
            P.op("vector", lambda e, tm2=tm2, tre=tre, d=d, j=j, dj=dj: e.scalar_tensor_tensor(out=Cl[:, d, j, 1, :], in0=tre, scalar=sm[:, 10, dj:dj + 1], in1=tm2[:],
                                                                                            op0=ALU.mult, op1=ALU.subtract),
                 reads=[Tpt, Tsm, Ttm2], writes=[TCl], acc=True)
    P.end_phase("ssm_prep")
    P.pop()

    yacc = C.hT[:].rearrange("p k s -> p (k s)").bitcast(F32).rearrange("p (t s) -> p t s", t=4)
    Tyacc = [Tile() for _ in range(4)]
    car = P.sb("car", [128, 2, 2, 16], F32)
    Tcar = [[Tile() for _ in range(4)] for _ in range(2)]
    P.op("gpsimd", lambda e: e.memset(car[:], 0.0), writes=[t for r in Tcar for t in r])
    ucr = Ring(P, "uc", 3, [128, 4, TC], BF16)
    rrer = Ring(P, "rre", 2, [128, 512], F32, psum=True)
    rimr = Ring(P, "rim", 2, [128, 512], F32, psum=True)
    ypr = Ring(P, "yp", 2, [128, 512], F32, psum=True)
    a_r = [Ring(P, "sa%d" % i, 1, [128, 4, TC], F32) for i in range(4)]
    m_r = [Ring(P, "smm%d" % i, 2, [128, 4, TC], F32) for i in range(2)]
    w_r = [Ring(P, "sw%d" % i, 2, [128, 4, TC], F32) for i in range(2)]
    b_r = [Ring(P, "sb%d" % i, 1, [128, 4, TC], F32) for i in range(4)]
    x_r = [Ring(P, "sx%d" % i, 2, [128, 4, TC], BF16) for i in range(2)]
    cp_r = Ring(P, "scp", 2, [128, 4, 4], F32)
    ys_r = Ring(P, "sys", 2, [128, TC], F32)
    ys2_r = Ring(P, "sys2", 2, [128, TC], F32)
    pending = []

    def flush():
        while pending:
            pending.pop(0)()

    for d in range(2):
        for n in range(NCH):
            ci = n if d == 0 else NCH - 1 - n
            t0 = ci * TC
            uc, Tuc = ucr.next()
            P.dma("sync", uc[:], C.usT[:, :, t0:t0 + TC].rearrange("g p s -> p g s"), writes=[Tuc], st=Tuc)
            for t in range(4):
                rre, Trre = rrer.next()
                rim, Trim = rimr.next()
                yp, Typ = ypr.next()
                rhs = uc[:, t, :] if d == 0 else uc[:, t, ::-1]
                for jj in range(4):
                    j = 4 * t + jj
                    mm(P, rre[:, jj * TC:(jj + 1) * TC], Bl[:, j, 0, :], rhs, True, True, [TBl, Tuc], [Trre])
                    mm(P, rim[:, jj * TC:(jj + 1) * TC], Bl[:, j, 1, :], rhs, True, True, [TBl, Tuc], [Trim])
                flush()
                dj0 = d * 16 + 4 * t
                c4 = ctab[:, dj0:dj0 + 4, 0:TC]
                s4 = stab[:, dj0:dj0 + 4, 0:TC]
                rre3 = rre[:].rearrange("p (a b) -> p a b", a=4)
                rim3 = rim[:].rearrange("p (a b) -> p a b", a=4)
                (a1, Ta1), (a2, Ta2), (a3, Ta3), (a4, Ta4) = [r.next() for r in a_r]
                P.op("vector", lambda e, a1=a1, rre3=rre3, c4=c4: e.tensor_mul(out=a1[:], in0=rre3, in1=c4), reads=[Trre, Ttab], writes=[Ta1])
                P.op("vector", lambda e, a2=a2, rim3=rim3, s4=s4: e.tensor_mul(out=a2[:], in0=rim3, in1=s4), reads=[Trim, Ttab], writes=[Ta2])
                P.op("vector", lambda e, a3=a3, rim3=rim3, c4=c4: e.tensor_mul(out=a3[:], in0=rim3, in1=c4), reads=[Trim, Ttab], writes=[Ta3])
                P.op("vector", lambda e, a4=a4, rre3=rre3, s4=s4: e.tensor_mul(out=a4[:], in0=rre3, in1=s4), reads=[Trre, Ttab], writes=[Ta4])
                (mre, Tmre), (mim, Tmim) = [r.next() for r in m_r]
                P.op("gpsimd", lambda e, mre=mre, a1=a1, a2=a2: e.tensor_add(out=mre[:], in0=a1[:], in1=a2[:]), reads=[Ta1, Ta2], writes=[Tmre])
                P.op("gpsimd", lambda e, mim=mim, a3=a3, a4=a4: e.tensor_sub(out=mim[:], in0=a3[:], in1=a4[:]), reads=[Ta3, Ta4], writes=[Tmim])
                (wre, Twre), (wim, Twim) = [r.next() for r in w_r]
                Tc_ = Tcar[d][t]
                for jj in range(4):
                    j = 4 * t + jj
                    dj = d * 16 + j
                    P.op("vector", lambda e, wre=wre, mre=mre, jj=jj, j=j, dj=dj, d=d: e.tensor_tensor_scan(
                        out=wre[:, jj, :], data0=rho[:, dj:dj + 1].to_broadcast([128, TC]), data1=mre[:, jj, :],
                        initial=car[:, d, 0, j:j + 1], op0=ALU.mult, op1=ALU.add), reads=[Tmre, Trho, Tc_], writes=[Twre], acc=(jj > 0))
                    P.op("vector", lambda e, wim=wim, mim=mim, jj=jj, j=j, dj=dj, d=d: e.tensor_tensor_scan(
                        out=wim[:, jj, :], data0=rho[:, dj:dj + 1].to_broadcast([128, TC]), data1=mim[:, jj, :],
                        initial=car[:, d, 1, j:j + 1], op0=ALU.mult, op1=ALU.add), reads=[Tmim, Trho, Tc_], writes=[Twim], acc=(jj > 0))
                cT = ctab[:, dj0:dj0 + 4, TC]
                sT = stab[:, dj0:dj0 + 4, TC]
                wlr = wre[:, :, TC - 1]
                wli = wim[:, :, TC - 1]
                cp, Tcp = cp_r.next()
                G_ = lambda fn, rd, wr, acc=False: P.op("gpsimd", fn, reads=rd, writes=wr, acc=acc)
                G_(lambda e, cp=cp, wlr=wlr, cT=cT: e.tensor_mul(out=cp[:, 0, :], in0=wlr, in1=cT), [Twre, Ttab], [Tcp])
                G_(lambda e, cp=cp, wli=wli, sT=sT: e.tensor_mul(out=cp[:, 1, :], in0=wli, in1=sT), [Twim, Ttab], [Tcp], True)
                G_(lambda e, cp=cp, wlr=wlr, sT=sT: e.tensor_mul(out=cp[:, 2, :], in0=wlr, in1=sT), [Twre, Ttab], [Tcp], True)
                G_(lambda e, cp=cp, wli=wli, cT=cT: e.tensor_mul(out=cp[:, 3, :], in0=wli, in1=cT), [Twim, Ttab], [Tcp], True)
                G_(lambda e, cp=cp, d=d, t=t: e.tensor_sub(out=car[:, d, 0, 4 * t:4 * t + 4], in0=cp[:, 0, :], in1=cp[:, 1, :]), [Tcp], [Tc_])
                G_(lambda e, cp=cp, d=d, t=t: e.tensor_add(out=car[:, d, 1, 4 * t:4 * t + 4], in0=cp[:, 2, :], in1=cp[:, 3, :]), [Tcp], [Tc_], True)
                (b1, Tb1), (b2, Tb2), (b3, Tb3), (b4, Tb4) = [r.next() for r in b_r]
                (xre, Txre), (xim, Txim) = [r.next() for r in x_r]
                G_(lambda e, b1=b1, wre=wre, c4=c4: e.tensor_mul(out=b1[:], in0=wre[:], in1=c4), [Twre, Ttab], [Tb1])
                G_(lambda e, b2=b2, wim=wim, s4=s4: e.tensor_mul(out=b2[:], in0=wim[:], in1=s4), [Twim, Ttab], [Tb2])
                G_(lambda e, xre=xre, b1=b1, b2=b2: e.tensor_sub(out=xre[:], in0=b1[:], in1=b2[:]), [Tb1, Tb2], [Txre])
                G_(lambda e, b3=b3, wre=wre, s4=s4: e.tensor_mul(out=b3[:], in0=wre[:], in1=s4), [Twre, Ttab], [Tb3])
                G_(lambda e, b4=b4, wim=wim, c4=c4: e.tensor_mul(out=b4[:], in0=wim[:], in1=c4), [Twim, Ttab], [Tb4])
                G_(lambda e, xim=xim, b3=b3, b4=b4: e.tensor_add(out=xim[:], in0=b3[:], in1=b4[:]), [Tb3, Tb4], [Txim])

                def later(d=d, t=t, t0=t0, yp=yp, Typ=Typ, xre=xre, Txre=Txre, xim=xim, Txim=Txim, uc=uc, Tuc=Tuc):
                    for jj in range(4):
                        j = 4 * t + jj
                        mm(P, yp[:, 0:TC], Cl[:, d, j, 0, :], xre[:, jj, :], jj == 0, False, [TCl, Txre], [Typ])
                        mm(P, yp[:, 0:TC], Cl[:, d, j, 1, :], xim[:, jj, :], False, jj == 3, [TCl, Txim], [Typ])
                    if d == 0:
                        P.op("scalar", lambda e: e.copy(out=yacc[:, t, t0:t0 + TC], in_=yp[:, 0:TC]), reads=[Typ], writes=[Tyacc[t]], acc=True)
                    else:
                        ys, Tys = ys_r.next()
                        ys2, Tys2 = ys2_r.next()
                        P.op("vector", lambda e: e.tensor_add(out=ys[:], in0=yp[:, 0:TC], in1=yacc[:, t, t0:t0 + TC][:, ::-1]),
                             reads=[Typ, Tyacc[t]], writes=[Tys])
                        P.op("vector", lambda e: e.scalar_tensor_tensor(out=ys2[:], in0=uc[:, t, ::-1], scalar=dcol[:, t:t + 1], in1=ys[:],
                                                                        op0=ALU.mult, op1=ALU.add), reads=[Tuc, Tdcol, Tys], writes=[Tys2])
                        P.op("scalar", lambda e: e.activation(out=C.sT[:, t, t0:t0 + TC][:, ::-1], in_=ys2[:], func=AF.Gelu_apprx_tanh),
                             reads=[Tys2], writes=[C.TsT[t][t0 // 512]], acc=True)
                pending.append(later)
    flush()
    P.end_phase("ssm")
    P.pop()


def phase_merge(P, C, l):
    W = C.W
    P.push()
    bcol = P.sb("bcol", [128, 16], F32)
    Tbcol = Tile()
    wst = Ring(P, "mws", 2, [128, 20, 128], F32)
    wbr = Ring(P, "mwb", 2, [128, 20, 128], BF16)
    gr = Ring(P, "mg", 2, [128, 3, S], BF16)
    psr = Ring(P, "mps", 6, [128, 512], F32, psum=True)
    load_cols(P, C, bcol[:], Tbcol, W["b_glu"][l].rearrange("(c p) -> c p", p=128), 16, psr, "bcol")
    tar = Ring(P, "mta", 2, [128, 512], F32)
    tbr = Ring(P, "mtb", 2, [128, 512], F32)
    sgr = Ring(P, "msg", 2, [128, 512], F32)
    ycr = Ring(P, "myc", 2, [128, 512], F32)
    for i in range(8):
        ws, Tws = wst.next()
        wb, Twb = wbr.next()
        cs = slice(i * 128, (i + 1) * 128)
        P.dma("sync", ws[:, 0:8, :], W["w_attn_br"][l][:, cs].rearrange("(kt p) c -> p kt c", p=128), writes=[Tws], st=Tws)
        P.dma("sync", ws[:, 8:12, :], W["w_fourier_br"][l][:, cs].rearrange("(kt p) c -> p kt c", p=128), writes=[Tws], st=Tws)
        P.dma("sync", ws[:, 12:16, :], W["w_glu"][l][:, cs].rearrange("(kt p) c -> p kt c", p=128), writes=[Tws], st=Tws)
        P.dma("sync", ws[:, 16:20, :], W["w_glu"][l][:, 1024 + i * 128:1024 + (i + 1) * 128].rearrange("(kt p) c -> p kt c", p=128),
              writes=[Tws], st=Tws)
        P.op("gpsimd", lambda e, ws=ws, wb=wb: e.tensor_copy(out=wb[:], in_=ws[:]), reads=[Tws], writes=[Twb])
        g, Tg = gr.next()
        for a in range(3):
            P.dma("sync", g[:, a, :], C.gT[a * 8 + i], writes=[Tg], st=Tg)
        for blk in range(4):
            tb = slice(blk * 512, (blk + 1) * 512)
            ya, Tya = psr.next()
            yb, Tyb = psr.next()
            z1, Tz1 = psr.next()
            z2, Tz2 = psr.next()
            for kt in range(8):
                mm(P, ya[:], wb[:, kt, :], C.oT[:, kt, tb], kt == 0, kt == 7, [Twb, C.ToT[kt][blk]], [Tya])
            for kt in range(4):
                mm(P, yb[:], wb[:, 8 + kt, :], C.fT[:, kt, tb], kt == 0, kt == 3, [Twb, C.TfT[kt][blk]], [Tyb])
            for kt in range(4):
                mm(P, z1[:], wb[:, 12 + kt, :], C.sT[:, kt, tb], kt == 0, kt == 3, [Twb, C.TsT[kt][blk]], [Tz1])
            for kt in range(4):
                mm(P, z2[:], wb[:, 16 + kt, :], C.sT[:, kt, tb], kt == 0, kt == 3, [Twb, C.TsT[kt][blk]], [Tz2])
            ta, Tta = tar.next()
            t2, Tt2 = tbr.next()
            sg, Tsg = sgr.next()
            yc, Tyc = ycr.next()
            P.op("vector", lambda e, ta=ta, ya=ya, g=g, tb=tb: e.tensor_mul(out=ta[:], in0=ya[:], in1=g[:, 0, tb]), reads=[Tya, Tg], writes=[Tta])
            P.op("vector", lambda e, t2=t2, yb=yb, g=g, tb=tb: e.tensor_mul(out=t2[:], in0=yb[:], in1=g[:, 1, tb]), reads=[Tyb, Tg], writes=[Tt2])
            P.op("scalar", lambda e, sg=sg, z2=z2, i=i: e.activation(out=sg[:], in_=z2[:], func=AF.Sigmoid, bias=bcol[:, 8 + i:9 + i]),
                 reads=[Tz2, Tbcol], writes=[Tsg])
            P.op("vector", lambda e, yc=yc, z1=z1, sg=sg, i=i: e.scalar_tensor_tensor(out=yc[:], in0=z1[:], scalar=bcol[:, i:i + 1], in1=sg[:],
                                                                                  op0=ALU.add, op1=ALU.mult), reads=[Tz1, Tsg, Tbcol], writes=[Tyc])
            P.op("gpsimd", lambda e, ta=ta, t2=t2: e.tensor_add(out=ta[:], in0=ta[:], in1=t2[:]), reads=[Tta, Tt2], writes=[Tta])
            P.op("gpsimd", lambda e, yc=yc, g=g, tb=tb: e.tensor_mul(out=yc[:], in0=yc[:], in1=g[:, 2, tb]), reads=[Tyc, Tg], writes=[Tyc])
            P.op("gpsimd", lambda e, ta=ta, yc=yc, i=i, tb=tb: e.tensor_add(out=C.hT[:, i, tb], in0=ta[:], in1=yc[:]), reads=[Tta, Tyc],
                 writes=[C.ThT[blk]], acc=True)
    P.end_phase("merge")
    P.pop()


def load_weight_bf(P, dst, Tdst, src2d, nk, ncol, st_ring, kchunk=8):
    for k0 in range(0, nk, kchunk):
        k1 = min(nk, k0 + kchunk)
        ws, Tws = st_ring.next()
        view = ws[:, 0:k1 - k0, 0:ncol]
        P.dma("sync", view, src2d[k0 * 128:k1 * 128, :].rearrange("(kt p) c -> p kt c", p=128), writes=[Tws], st=Tws)
        P.op("gpsimd", lambda e, view=view, k0=k0, k1=k1: e.tensor_copy(out=dst[:, k0:k1, :], in_=view), reads=[Tws], writes=[Tdst], acc=True)


def phase_wout(P, C, l, b, xsrc):
    P.push()
    wo = P.sb("wo", [128, 8, D], BF16)
    Two = Tile()
    st = Ring(P, "wos", 2, [128, 8, 512], F32)
    for cb in range(2):
        ws, Tws = st.next()
        P.dma("sync", ws[:], C.W["w_out"][l][:, cb * 512:(cb + 1) * 512].rearrange("(kt p) c -> p kt c", p=128), writes=[Tws], st=Tws)
        P.op("gpsimd", lambda e, ws=ws, cb=cb: e.tensor_copy(out=wo[:, :, cb * 512:(cb + 1) * 512], in_=ws[:]), reads=[Tws], writes=[Two], acc=True)
    xr = Ring(P, "wx", 2, [128, D], F32)
    xnr = Ring(P, "wxn", 2, [128, D], F32)
    psr = Ring(P, "wps", 4, [128, 512], F32, psum=True)
    for tt in range(NT):
        xt, Tx = xr.next()
        xn, Txn = xnr.next()
        ts = slice(tt * 128, (tt + 1) * 128)
        P.dma("sync", xt[:], xsrc[ts, :], writes=[Tx], st=Tx)
        for cb in range(2):
            cs = slice(cb * 512, (cb + 1) * 512)
            ps, Tps = psr.next()
            for kt in range(8):
                mm(P, ps[:], C.hT[:, kt, ts], wo[:, kt, cs], kt == 0, kt == 7, [C.ThT[tt // 4], Two], [Tps])
            P.op("vector", lambda e, xn=xn, ps=ps, xt=xt, cs=cs: e.tensor_add(out=xn[:, cs], in0=ps[:], in1=xt[:, cs]), reads=[Tps, Tx], writes=[Txn],
                 acc=(cb > 0))
        P.dma("gpsimd", C.xres[b][ts, :], xn[:], reads=[Txn], st=Txn)
    P.end_phase("wout")
    P.pop()


def phase_ffn(P, C, l, b):
    W = C.W
    P.push()
    aT = P.sb("aT", [128, 22, S], BF16)
    TaT = [Tile() for _ in range(4)]
    P.push()
    cw = P.sb("cw", [128, 4, 22], F32)
    Tcw = Tile()
    craw = P.sb("craw", [88, 128], F32)
    Tcraw = Tile()
    for k in range(3):
        P.dma("sync", craw[k * 22:(k + 1) * 22, :], W["conv_w"][l][k].rearrange("(i p) -> i p", p=128), writes=[Tcraw], st=Tcraw)
    P.dma("sync", craw[66:88, :], W["conv_b"][l].rearrange("(i p) -> i p", p=128), writes=[Tcraw], st=Tcraw)
    cps = Ring(P, "cwps", 1, [128, 512], F32, psum=True)
    cp_, Tcp_ = cps.next()
    P.op("tensor", lambda e: e.transpose(cp_[:, 0:88], craw[0:88, :], C.idf[0:88, 0:88]), reads=[Tcraw, C.Tidf], writes=[Tcp_])
    P.op("vector", lambda e: e.tensor_copy(out=cw[:].rearrange("p a b -> p (a b)"), in_=cp_[:, 0:88]), reads=[Tcp_], writes=[Tcw])
    wst = Ring(P, "fws", 2, [128, 16, 128], F32)
    wbr = Ring(P, "fwb", 2, [128, 16, 128], BF16)
    gbr = Ring(P, "fgb", 2, [128, S + 2], F32)
    for gb_, Tgb_ in gbr.bufs:
        P.op("gpsimd", lambda e, gb_=gb_: e.memset(gb_[:], 0.0), writes=[Tgb_])
    vbr = Ring(P, "fvb", 2, [128, S], BF16)
    cbr = Ring(P, "fcb", 2, [128, S], F32)
    glr = Ring(P, "fgl", 1, [128, S], F32)
    psr = Ring(P, "fps", 6, [128, 512], F32, psum=True)
    for i in range(22):
        ws, Tws = wst.next()
        wb, Twb = wbr.next()
        P.dma("sync", ws[:, 0:8, :], W["w_up"][l][:, i * 128:(i + 1) * 128].rearrange("(kt p) c -> p kt c", p=128), writes=[Tws], st=Tws)
        P.dma("sync", ws[:, 8:16, :], W["w_up"][l][:, 2816 + i * 128:2816 + (i + 1) * 128].rearrange("(kt p) c -> p kt c", p=128),
              writes=[Tws], st=Tws)
        P.op("gpsimd", lambda e, ws=ws, wb=wb: e.tensor_copy(out=wb[:], in_=ws[:]), reads=[Tws], writes=[Twb])
        gb, Tgb = gbr.next()
        vb, Tvb = vbr.next()
        for blk in range(4):
            tb = slice(blk * 512, (blk + 1) * 512)
            pg, Tpg = psr.next()
            pv, Tpv = psr.next()
            for kt in range(8):
                mm(P, pg[:], wb[:, kt, :], C.hT[:, kt, tb], kt == 0, kt == 7, [Twb, C.ThT[blk]], [Tpg])
            for kt in range(8):
                mm(P, pv[:], wb[:, 8 + kt, :], C.hT[:, kt, tb], kt == 0, kt == 7, [Twb, C.ThT[blk]], [Tpv])
            P.op("scalar", lambda e, gb=gb, pg=pg, blk=blk: e.copy(out=gb[:, 1 + blk * 512:1 + (blk + 1) * 512], in_=pg[:]), reads=[Tpg], writes=[Tgb],
                 acc=(blk > 0))
            P.op("vector", lambda e, vb=vb, pv=pv, tb=tb: e.tensor_copy(out=vb[:, tb], in_=pv[:]), reads=[Tpv], writes=[Tvb], acc=(blk > 0))
        c1, Tc1 = cbr.next()
        c2, Tc2 = cbr.next()
        gl, Tgl = glr.next()
        P.op("vector", lambda e, c1=c1, gb=gb, i=i: e.tensor_scalar(out=c1[:], in0=gb[:, 1:S + 1], scalar1=cw[:, 1, i:i + 1], scalar2=cw[:, 3, i:i + 1],
                                                                   op0=ALU.mult, op1=ALU.add), reads=[Tgb, Tcw], writes=[Tc1])
        P.op("vector", lambda e, c2=c2, c1=c1, gb=gb, i=i: e.scalar_tensor_tensor(out=c2[:], in0=gb[:, 0:S], scalar=cw[:, 0, i:i + 1], in1=c1[:],
                                                                              op0=ALU.mult, op1=ALU.add), reads=[Tgb, Tcw, Tc1], writes=[Tc2])
        P.op("vector", lambda e, c2=c2, c1=c1, gb=gb, i=i: e.scalar_tensor_tensor(out=c1[:], in0=gb[:, 2:S + 2], scalar=cw[:, 2, i:i + 1], in1=c2[:],
                                                                              op0=ALU.mult, op1=ALU.add), reads=[Tgb, Tcw, Tc2], writes=[Tc1])
        P.op("scalar", lambda e, gl=gl, c1=c1: e.activation(out=gl[:], in_=c1[:], func=AF.Gelu_apprx_tanh), reads=[Tc1], writes=[Tgl])
        P.op("gpsimd", lambda e, gl=gl, vb=vb, i=i: e.tensor_mul(out=aT[:, i, :], in0=gl[:], in1=vb[:]), reads=[Tgl, Tvb], writes=TaT, acc=True)
    P.end_phase("ffn_up")
    P.pop()
    P.push()
    wd = P.sb("wd", [128, 22, 512], BF16)
    Twd = Tile()
    st = Ring(P, "fds", 2, [128, 8, 512], F32)
    xr = Ring(P, "fx", 2, [128, 512], F32)
    xnr = Ring(P, "fxn", 2, [128, 512], F32)
    psr = Ring(P, "dps", 3, [128, 512], F32, psum=True)
    for cb in range(2):
        cs = slice(cb * 512, (cb + 1) * 512)
        load_weight_bf(P, wd, Twd, W["w_down"][l][:, cs], 22, 512, st)
        for tt in range(NT):
            ts = slice(tt * 128, (tt + 1) * 128)
            xt, Tx = xr.next()
            xn, Txn = xnr.next()
            P.dma("sync", xt[:], C.xres[b][ts, cs], writes=[Tx], st=Tx)
            ps, Tps = psr.next()
            for kt in range(22):
                mm(P, ps[:], aT[:, kt, ts], wd[:, kt, :], kt == 0, kt == 21, [TaT[tt // 4], Twd], [Tps])
            P.op("vector", lambda e, xn=xn, ps=ps, xt=xt: e.tensor_add(out=xn[:], in0=ps[:], in1=xt[:]), reads=[Tps, Tx], writes=[Txn])
            P.dma("gpsimd", C.xres[b][ts, cs], xn[:], reads=[Txn], st=Txn)
    P.end_phase("ffn_down")
    P.pop()
    P.pop()


def phase_ple(P, C, l, b):
    W = C.W
    P.push()
    wg = P.sb("wg", [128, 8, D], BF16)
    Twg = Tile()
    wp = P.sb("wp", [128, 2, D], BF16)
    Twp = Tile()
    st = Ring(P, "pls", 2, [128, 8, 512], F32)
    for cb in range(2):
        cs = slice(cb * 512, (cb + 1) * 512)
        ws, Tws = st.next()
        P.dma("sync", ws[:], W["w_ple_gate"][l][:, cs].rearrange("(kt p) c -> p kt c", p=128), writes=[Tws], st=Tws)
        P.op("gpsimd", lambda e, ws=ws, cs=cs: e.tensor_copy(out=wg[:, :, cs], in_=ws[:]), reads=[Tws], writes=[Twg], acc=True)
    for cb in range(2):
        cs = slice(cb * 512, (cb + 1) * 512)
        ws, Tws = st.next()
        P.dma("sync", ws[:, 0:2, :], W["w_ple"][l][:, cs].rearrange("(kt p) c -> p kt c", p=128), writes=[Tws], st=Tws)
        P.op("gpsimd", lambda e, ws=ws, cs=cs: e.tensor_copy(out=wp[:, :, cs], in_=ws[:, 0:2, :]), reads=[Tws], writes=[Twp], acc=True)
    xr = Ring(P, "px", 2, [128, D], F32)
    xnr = Ring(P, "pxn", 2, [128, D], F32)
    pfr = Ring(P, "ppf", 2, [128, 256], F32)
    pbr = Ring(P, "ppb", 2, [128, 256], BF16)
    ptr = Ring(P, "ppt", 2, [128, 1024], BF16, psum=True)
    pTr = Ring(P, "ppT", 2, [128, 2, 128], BF16)
    psr = Ring(P, "pps", 4, [128, 512], F32, psum=True)
    sgr = Ring(P, "psg", 2, [128, 512], F32)
    prr = Ring(P, "ppr", 2, [128, 512], F32)
    for tt in range(NT):
        ts = slice(tt * 128, (tt + 1) * 128)
        xt, Tx = xr.next()
        xn, Txn = xnr.next()
        pf, Tpf = pfr.next()
        pb, Tpb = pbr.next()
        pt, Tpt = ptr.next()
        pT, TpT = pTr.next()
        P.dma("sync", xt[:], C.xres[b][ts, :], writes=[Tx], st=Tx)
        P.dma("sync", pf[:], C.p[l, b][ts, :], writes=[Tpf], st=Tpf)
        P.op("gpsimd", lambda e, pb=pb, pf=pf: e.tensor_copy(out=pb[:], in_=pf[:]), reads=[Tpf], writes=[Tpb])
        for kt in range(2):
            P.op("tensor", lambda e, pt=pt, pb=pb, kt=kt: e.transpose(pt[:, kt * 128:(kt + 1) * 128], pb[:, kt * 128:(kt + 1) * 128], C.ident),
                 reads=[Tpb, C.Tcbs], writes=[Tpt], acc=(kt > 0))
        P.op("scalar", lambda e, pT=pT, pt=pt: e.copy(out=pT[:].rearrange("p a b -> p (a b)"), in_=pt[:, 0:256]), reads=[Tpt], writes=[TpT])
        for cb in range(2):
            cs = slice(cb * 512, (cb + 1) * 512)
            pg, Tpg = psr.next()
            pe, Tpe = psr.next()
            for kt in range(8):
                mm(P, pg[:], C.hT[:, kt, ts], wg[:, kt, cs], kt == 0, kt == 7, [C.ThT[tt // 4], Twg], [Tpg])
            for kt in range(2):
                mm(P, pe[:], pT[:, kt, :], wp[:, kt, cs], kt == 0, kt == 1, [TpT, Twp], [Tpe])
            sg, Tsg = sgr.next()
            pr, Tpr = prr.next()
            P.op("scalar", lambda e, sg=sg, pg=pg: e.activation(out=sg[:], in_=pg[:], func=AF.Sigmoid), reads=[Tpg], writes=[Tsg])
            P.op("vector", lambda e, pr=pr, pe=pe, sg=sg: e.tensor_mul(out=pr[:], in0=pe[:], in1=sg[:]), reads=[Tpe, Tsg], writes=[Tpr])
            P.op("gpsimd", lambda e, xn=xn, xt=xt, pr=pr, cs=cs: e.tensor_add(out=xn[:, cs], in0=xt[:, cs], in1=pr[:]), reads=[Tx, Tpr], writes=[Txn],
                 acc=(cb > 0))
        P.dma("gpsimd", C.xres[b][ts, :], xn[:], reads=[Txn], st=Txn)
    P.end_phase("ple")
    P.pop()


def phase_final(P, C, b):
    P.push()
    gb = P.sb("fgb", [128, D], F32)
    Tgb = Tile()
    P.dma("sync", gb[:], C.W["final_norm_g"].partition_broadcast(128), writes=[Tgb], st=Tgb)
    xr = Ring(P, "fnx", 2, [128, D], F32)
    jr = Ring(P, "fnj", 2, [128, D], BF16)
    sr = Ring(P, "fns", 2, [128, 2], F32)
    orr = Ring(P, "fno", 2, [128, D], F32)
    for tt in range(NT):
        ts = slice(tt * 128, (tt + 1) * 128)
        xt, Tx = xr.next()
        jk, Tj = jr.next()
        ss, Tss = sr.next()
        o, To = orr.next()
        P.dma("sync", xt[:], C.xres[b][ts, :], writes=[Tx], st=Tx)
        P.op("scalar", lambda e, jk=jk, xt=xt, ss=ss: e.activation(out=jk[:], in_=xt[:], func=AF.Square, accum_out=ss[:, 0:1]), reads=[Tx], writes=[Tj, Tss])
        P.op("vector", lambda e, ss=ss: e.tensor_scalar(out=ss[:, 1:2], in0=ss[:, 0:1], scalar1=1.0 / D, scalar2=EPS, op0=ALU.mult, op1=ALU.add),
             reads=[Tss], writes=[Tss])
        P.op("scalar", lambda e, ss=ss: e.activation(out=ss[:, 1:2], in_=ss[:, 1:2], func=AF.Sqrt), reads=[Tss], writes=[Tss])
        P.op("vector", lambda e, ss=ss: e.reciprocal(out=ss[:, 1:2], in_=ss[:, 1:2]), reads=[Tss], writes=[Tss])
        P.op("vector", lambda e, o=o, xt=xt, ss=ss: e.scalar_tensor_tensor(out=o[:], in0=xt[:], scalar=ss[:, 1:2], in1=gb[:], op0=ALU.mult, op1=ALU.mult),
             reads=[Tx, Tss, Tgb], writes=[To])
        P.dma("gpsimd", C.y[b][ts, :], o[:], reads=[To], st=To)
    P.end_phase("final")
    P.pop()
```

```python
import math
from contextlib import ExitStack
import numpy as np
import ml_dtypes
import concourse.bass as bass
import concourse.mybir as mybir
from concourse.bass_utils import run_bass_kernel_spmd

F32 = mybir.dt.float32
BF16 = mybir.dt.bfloat16
I32 = mybir.dt.int32
ALU = mybir.AluOpType
AF = mybir.ActivationFunctionType

S = 2048
D = 1024
NT = S // 128
EPS = 1e-6
TWO_PI = 2.0 * math.pi
SIN_SCALE = TWO_PI * (1.0 - 2e-6)


class Tile:
    __slots__ = ("name", "w", "r", "dsem", "dcnt", "excl")

    def __init__(self, name="", excl=False):
        self.name = name
        self.excl = excl
        self.w = {}
        self.r = {}
        self.dsem = None
        self.dcnt = 0


class Prog:
    ENGS = ("sync", "scalar", "vector", "gpsimd", "tensor")

    def __init__(self, nc):
        self.nc = nc
        self.st = ExitStack()
        self.scopes = [self.st]
        self.streams = {e: [] for e in self.ENGS}
        self.seen = {e: {} for e in self.ENGS}
        self.total = {}
        self.bar_seen = {}
        self.sems = {}
        for e in self.ENGS:
            self.sems[("e", e)] = self.st.enter_context(nc.semaphore("c_" + e))
            self.total[("e", e)] = 0
        self.sems["bar"] = self.st.enter_context(nc.semaphore("bar"))
        self.nbar = 0
        self.nsem = 0
        self.free_dsems = {}
        self.phase_tiles = []
        self.uid = 0
        self.verbose = False

    def push(self):
        s = ExitStack()
        self.scopes.append(s)
        return s

    def pop(self):
        self.scopes.pop().close()

    def sb(self, name, shape, dt):
        self.uid += 1
        return self.scopes[-1].enter_context(
            self.nc.sbuf_tensor("%s_%d" % (name, self.uid), list(shape), dt))

    def ps(self, name, shape, dt=F32):
        self.uid += 1
        return self.scopes[-1].enter_context(
            self.nc.psum_tensor("%s_%d" % (name, self.uid), list(shape), dt))

    def _dsem(self, tile, q):
        if tile.dsem is None:
            tile.dsem = {}
        if q not in tile.dsem:
            pool = self.free_dsems.setdefault(q, [])
            if pool:
                key = pool.pop()
            else:
                key = ("d", self.nsem)
                self.nsem += 1
                self.sems[key] = self.st.enter_context(self.nc.semaphore("d%d" % key[1]))
                self.total[key] = 0
            tile.dsem[q] = key
            self.phase_tiles.append((tile, q))
        return tile.dsem[q]

    def _collect(self, eng, reads, writes, acc):
        need = {}
        own = ("e", eng)
        for t in reads:
            for k, v in t.w.items():
                if need.get(k, 0) < v:
                    need[k] = v
            if t.excl:
                for k, v in t.r.items():
                    if k != own and need.get(k, 0) < v:
                        need[k] = v
        for t in writes:
            for d in (t.w, t.r):
                for k, v in d.items():
                    if acc and k == own:
                        continue
                    if need.get(k, 0) < v:
                        need[k] = v
        seen = self.seen[eng]
        waits = []
        for k, v in need.items():
            if seen.get(k, 0) < v:
                seen[k] = v
                waits.append((k, v))
        return waits

    @staticmethod
    def _finish(ev, reads, writes):
        k, v = ev
        for t in writes:
            t.w = {k: v}
            t.r = {}
        for t in reads:
            if t.r.get(k, 0) < v:
                t.r[k] = v

    def op(self, eng, fn, reads=(), writes=(), acc=False):
        waits = self._collect(eng, reads, writes, acc)
        key = ("e", eng)
        self.total[key] += 1
        ev = (key, self.total[key])
        self.streams[eng].append((waits, fn, (key, 1)))
        self._finish(ev, reads, writes)

    def dma(self, q, out, in_, reads=(), writes=(), st=None, **kw):
        waits = self._collect(q, reads, writes, False)
        key = self._dsem(st, q)
        self.total[key] += 16
        ev = (key, self.total[key])
        self.streams[q].append(
            (waits, lambda e: e.dma_start(out=out, in_=in_, **kw), (key, 16)))
        self._finish(ev, reads, writes)

    def barrier(self):
        waits = []
        for k, v in self.total.items():
            if self.bar_seen.get(k, 0) < v:
                self.bar_seen[k] = v
                waits.append((k, v))
        self.nbar += 1
        n = self.nbar
        bar = self.sems["bar"]
        self.streams["sync"].append((waits, lambda e: e.sem_inc(bar, 1), None))
        for e in self.ENGS:
            if e != "sync":
                self.streams[e].append(([("bar", n)], None, None))
            for k, v in self.total.items():
                self.seen[e][k] = v

    def emit(self):
        nc = self.nc
        sems = self.sems

        def run(eng_name):
            ops = self.streams[eng_name]

            def body(e):
                for waits, fn, inc in ops:
                    for k, v in waits:
                        e.wait_ge(sems[k], v)
                    if fn is not None:
                        ins = fn(e)
                        if inc is not None:
                            ins.then_inc(sems[inc[0]], inc[1])
            return body

        with nc.Block() as block:
            block.sync(run("sync"))
            block.scalar(run("scalar"))
            block.vector(run("vector"))
            block.gpsimd(run("gpsimd"))
            block.tensor(run("tensor"))
        self.streams = {e: [] for e in self.ENGS}

    def end_phase(self, name=""):
        if self.verbose:
            print("phase", name, "sbuf remaining", self.nc.sbuf_bytes_remaining,
                  {e: len(v) for e, v in self.streams.items()}, "nsem", self.nsem, flush=True)
        self.barrier()
        self.emit()
        for t, q in self.phase_tiles:
            self.free_dsems[q].append(t.dsem.pop(q))
        self.phase_tiles = []

    def close(self):
        self.st.close()


class Ring:
    def __init__(self, P, name, n, shape, dt, psum=False):
        self.bufs = []
        for i in range(n):
            t = P.ps(name, shape, dt) if psum else P.sb(name, shape, dt)
            self.bufs.append((t, Tile(name + str(i), excl=psum)))
        self.i = 0

    def next(self):
        b = self.bufs[self.i % len(self.bufs)]
        self.i += 1
        return b


def host_consts():
    cb = np.zeros((128, 3, 128), np.float32)
    cb[:, 0, :] = np.eye(128)
    cb[:, 1, :] = 1.0
    for dst in range(128):
        d = dst % 64
        if d < 8:
            cb[dst + 8, 2, dst] = -1.0
        elif d < 16:
            cb[dst - 8, 2, dst] = 1.0
    cf = np.zeros((128, 16), np.float32)
    inv = 500000.0 ** (-np.arange(0, 16, 2, dtype=np.float64) / 16.0)
    for r in range(128):
        d = r % 64
        cf[r, 0] = inv[d % 8] / TWO_PI if d < 16 else 0.0
    cf[:, 1] = np.arange(128)
    for a in range(4):
        for gl in range(2):
            lo = 32 * a + 16 * gl
            cf[lo:lo + 16, 2 + a * 2 + gl] = 1.0
    return cb.astype(ml_dtypes.bfloat16), cf


WNAMES = [
    ("norm_mix_g", [2, 1024]), ("w_in", [2, 1024, 7168]),
    ("lambda_q1", [2, 64]), ("lambda_k1", [2, 64]), ("lambda_q2", [2, 64]), ("lambda_k2", [2, 64]),
    ("attn_subln_g", [2, 128]), ("w_attn_br", [2, 1024, 1024]), ("w_fourier_br", [2, 512, 1024]),
    ("ssm_a_re", [2, 2, 32, 64]), ("ssm_a_im", [2, 2, 32, 64]), ("ssm_log_dt", [2, 2, 32]),
    ("ssm_b_re", [2, 32, 64, 16]), ("ssm_b_im", [2, 32, 64, 16]),
    ("ssm_c_re", [2, 32, 16, 64]), ("ssm_c_im", [2, 32, 16, 64]), ("ssm_d", [2, 512]),
    ("w_glu", [2, 512, 2048]), ("b_glu", [2, 2048]), ("w_out", [2, 1024, 1024]),
    ("norm_ffn_g", [2, 1024]), ("w_up", [2, 1024, 5632]), ("conv_w", [2, 3, 2816]),
    ("conv_b", [2, 2816]), ("w_down", [2, 2816, 1024]), ("norm_ple_g", [2, 1024]),
    ("w_ple_gate", [2, 1024, 1024]), ("w_ple", [2, 256, 1024]), ("final_norm_g", [1024]),
]


class Ctx:
    pass


def dump(P, C, name, sbt, shape, tiles, dt=BF16):
    d = P.nc.dram_tensor("dbg_" + name, list(shape), dt, kind="ExternalOutput").ap()
    T = Tile()
    P.dma("sync", d, sbt[:], reads=list(tiles), st=T)
    P.end_phase("dump")


def mm(P, out, lhsT, rhs, start, stop, reads, writes):
    P.op("tensor", lambda e: e.matmul(out, lhsT=lhsT, rhs=rhs, start=start, stop=stop),
         reads=reads, writes=writes, acc=not start)


def frac_turns(P, u, n, tag):
    ut, Tu = u
    ki = P.sb("ki" + tag, [128, n], I32)
    Tki = Tile()
    kf = P.sb("kf" + tag, [128, n], F32)
    Tkf = Tile()
    f = P.sb("fr" + tag, [128, n], F32)
    Tf = Tile()
    P.op("vector", lambda e: e.tensor_copy(out=ki[:], in_=ut[:]), reads=[Tu], writes=[Tki])
    P.op("gpsimd", lambda e: e.tensor_copy(out=kf[:], in_=ki[:]), reads=[Tki], writes=[Tkf])
    P.op("vector", lambda e: e.tensor_sub(out=f[:], in0=ut[:], in1=kf[:]), reads=[Tu, Tkf], writes=[Tf])
    return f, Tf


def sincos_from_frac(P, f, Tf, n, sin_out, Tsin, cos_out, Tcos, tag, sin_sign=1.0):
    g = P.sb("wg" + tag, [128, n], F32)
    Tg = Tile()
    f3 = P.sb("f3" + tag, [128, n], F32)
    Tf3 = Tile()
    P.op("scalar", lambda e: e.activation(out=sin_out, in_=f[:], func=AF.Sin, scale=SIN_SCALE * sin_sign),
         reads=[Tf], writes=[Tsin])
    P.op("vector", lambda e: e.tensor_scalar(out=g[:], in0=f[:], scalar1=0.25, scalar2=0.5,
                                             op0=ALU.add, op1=ALU.is_gt), reads=[Tf], writes=[Tg])
    P.op("vector", lambda e: e.scalar_tensor_tensor(out=f3[:], in0=f[:], scalar=0.25, in1=g[:],
                                                    op0=ALU.add, op1=ALU.subtract),
         reads=[Tf, Tg], writes=[Tf3])
    P.op("scalar", lambda e: e.activation(out=cos_out, in_=f3[:], func=AF.Sin, scale=SIN_SCALE),
         reads=[Tf3], writes=[Tcos])


def build(n_layers=2, n_seq=2, debug=False, stop=None):
    nc = bass.Bass("TRN2", target_bir_lowering=False)
    C = Ctx()
    import os
    C.norot = 'norot' in os.environ.get('KDBG', '')
    C.nosig = 'nosig' in os.environ.get('KDBG', '')

    def din(name, shape, dt=F32):
        return nc.dram_tensor(name, list(shape), dt, kind="ExternalInput").ap()

    C.x = din("x", [2, S, D])
    C.p = din("p", [2, 2, S, 256])
    C.pos = din("pos", [2, S], I32)
    C.cb = din("cb", [128, 3, 128], BF16)
    C.cf = din("cf", [128, 16])
    C.W = {n: din(n, s) for n, s in WNAMES}
    C.y = nc.dram_tensor("y", [2, S, D], F32, kind="ExternalOutput").ap()
    skind = "ExternalOutput" if debug else "Internal"

    def scratch(name, shape, dt):
        return nc.dram_tensor(name, list(shape), dt, kind=skind).ap()

    C.xres = scratch("xres", [2, S, D], F32)
    C.QT = scratch("QT", [8, 128, S], BF16)
    C.KT = scratch("KT", [8, 128, S], BF16)
    C.V = scratch("V", [S, 1024], BF16)
    C.ufT = scratch("ufT", [4, 128, S], BF16)
    C.usT = scratch("usT", [4, 128, S], BF16)
    C.gT = scratch("gT", [24, 128, S], BF16)
    C.dftc = scratch("dftc", [S, S], BF16)
    C.dfts = scratch("dfts", [S, S], BF16)
    C.dbg = {}

    P = Prog(nc)
    P.verbose = debug
    C.P = P
    C.cbs = P.sb("cbs", [128, 3, 128], BF16)
    C.Tcbs = Tile("cbs")
    C.cfs = P.sb("cfs", [128, 16], F32)
    C.Tcfs = Tile("cfs")
    P.dma("sync", C.cbs[:], C.cb, writes=[C.Tcbs], st=C.Tcbs)
    P.dma("sync", C.cfs[:], C.cf, writes=[C.Tcfs], st=C.Tcfs)
    C.ident = C.cbs[:, 0, :]
    C.ones = C.cbs[:, 1, :]
    C.Pm = C.cbs[:, 2, :]
    C.hT = P.sb("hT", [128, 8, S], BF16)
    C.ThT = [Tile("hT%d" % i) for i in range(4)]
    C.cc = P.sb("cc", [128, 256], BF16)
    C.Tcc = Tile("cc")
    C.idf = P.sb("idf", [128, 128], F32)
    C.Tidf = Tile("idf")
    P.op("vector", lambda e: e.tensor_copy(out=C.idf[:], in_=C.ident), reads=[C.Tcbs], writes=[C.Tidf])
    P.end_phase("consts")
    phase_tables(P, C)

    done = False
    for b in range(n_seq):
        for l in range(n_layers):
            xsrc = C.x[b] if l == 0 else C.xres[b]
            phase_norm(P, C, xsrc, C.W["norm_mix_g"][l])
            phase_win(P, C, l, b)
            if stop == "A":
                done = True
                break
            P.push()
            C.oT = P.sb("oT", [128, 8, S], BF16)
            C.ToT = [[Tile() for _ in range(4)] for _ in range(8)]
            C.fT = P.sb("fT", [128, 4, S], BF16)
            C.TfT = [[Tile() for _ in range(4)] for _ in range(4)]
            C.sT = P.sb("sT", [128, 4, S], BF16)
            C.TsT = [[Tile() for _ in range(4)] for _ in range(4)]
            if stop != "C":
                phase_attn(P, C, l)
            if stop == "B":
                dump(P, C, "oT", C.oT, [128, 8, S], [t for r in C.ToT for t in r])
                done = True
            if not done:
                phase_fourier(P, C)
            if stop == "C":
                dump(P, C, "fT", C.fT, [128, 4, S], [t for r in C.TfT for t in r])
                done = True
            if not done:
                phase_ssm(P, C, l)
            if stop == "D":
                dump(P, C, "sT", C.sT, [128, 4, S], [t for r in C.TsT for t in r])
                done = True
            if not done:
                phase_merge(P, C, l)
            P.pop()
            if done:
                break
            if stop == "E":
                dump(P, C, "mT", C.hT, [128, 8, S], C.ThT)
                done = True
                break
            phase_wout(P, C, l, b, xsrc)
            if stop == "F":
                done = True
                break
            phase_norm(P, C, C.xres[b], C.W["norm_ffn_g"][l])
            phase_ffn(P, C, l, b)
            if stop == "H":
                done = True
                break
            phase_norm(P, C, C.xres[b], C.W["norm_ple_g"][l])
            phase_ple(P, C, l, b)
        if done:
            break
        phase_final(P, C, b)
    P.close()
    return nc, C


def phase_norm(P, C, xsrc, gain):
    P.push()
    gb = P.sb("gb", [128, D], F32)
    Tgb = Tile()
    P.dma("sync", gb[:], gain.partition_broadcast(128), writes=[Tgb], st=Tgb)
    xr = Ring(P, "xt", 2, [128, D], F32)
    jr = Ring(P, "junk", 2, [128, D], BF16)
    sr = Ring(P, "ss", 2, [128, 2], F32)
    hr = Ring(P, "hb", 2, [128, D], BF16)
    pr = Ring(P, "pt", 2, [128, D], BF16, psum=True)
    for tt in range(NT):
        xt, Tx = xr.next()
        P.dma("sync", xt[:], xsrc[tt * 128:(tt + 1) * 128, :], writes=[Tx], st=Tx)
        norm_tile(P, C, xt, Tx, gb, Tgb, tt, jr, sr, hr, pr)
    P.end_phase()
    P.pop()


def norm_tile(P, C, xt, Tx, gb, Tgb, tt, jr, sr, hr, pr):
    jk, Tj = jr.next()
    ss, Tss = sr.next()
    hb, Thb = hr.next()
    pt, Tpt = pr.next()
    P.op("scalar", lambda e: e.activation(out=jk[:], in_=xt[:], func=AF.Square, accum_out=ss[:, 0:1]),
         reads=[Tx], writes=[Tj, Tss])
    P.op("vector", lambda e: e.tensor_scalar(out=ss[:, 1:2], in0=ss[:, 0:1], scalar1=1.0 / D, scalar2=EPS,
                                             op0=ALU.mult, op1=ALU.add), reads=[Tss], writes=[Tss])
    P.op("scalar", lambda e: e.activation(out=ss[:, 1:2], in_=ss[:, 1:2], func=AF.Sqrt), reads=[Tss], writes=[Tss])
    P.op("vector", lambda e: e.reciprocal(out=ss[:, 1:2], in_=ss[:, 1:2]), reads=[Tss], writes=[Tss])
    P.op("vector", lambda e: e.scalar_tensor_tensor(out=hb[:], in0=xt[:], scalar=ss[:, 1:2], in1=gb[:],
                                                    op0=ALU.mult, op1=ALU.mult),
         reads=[Tx, Tss, Tgb], writes=[Thb])
    for kt in range(8):
        P.op("tensor", (lambda kt: lambda e: e.transpose(pt[:, kt * 128:(kt + 1) * 128],
                                                          hb[:, kt * 128:(kt + 1) * 128], C.ident))(kt),
             reads=[Thb, C.Tcbs], writes=[Tpt], acc=(kt > 0))
    Th = C.ThT[tt // 4]
    P.op("scalar", lambda e: e.copy(out=C.hT[:, :, tt * 128:(tt + 1) * 128],
                                    in_=pt[:].rearrange("p (k t) -> p k t", k=8)),
         reads=[Tpt], writes=[Th], acc=True)


def phase_win(P, C, l, b):
    P.push()
    posi = P.sb("posi", [128, S], I32)
    Tposi = Tile()
    P.dma("sync", posi[:], C.pos[b].partition_broadcast(128), writes=[Tposi], st=Tposi)
    u = P.sb("ropeu", [128, S], F32)
    Tu = Tile()
    P.op("vector", lambda e: e.tensor_copy(out=u[:], in_=posi[:]), reads=[Tposi], writes=[Tu])
    P.op("vector", lambda e: e.tensor_scalar(out=u[:], in0=u[:], scalar1=C.cfs[:, 0:1], scalar2=None,
                                             op0=ALU.mult), reads=[Tu, C.Tcfs], writes=[Tu])
    f, Tf = frac_turns(P, (u, Tu), S, "rope")
    crot = P.sb("crot", [128, S], F32)
    srot = P.sb("srot", [128, S], F32)
    Tcr, Tsr = Tile(), Tile()
    sincos_from_frac(P, f, Tf, S, srot[:], Tsr, crot[:], Tcr, "rope")

    W = C.W["w_in"][l]
    wst = Ring(P, "wst", 2, [128, 8, 512], F32)
    wbf = Ring(P, "wbf", 2, [128, 8, 512], BF16)
    acc = Ring(P, "acc", 3, [128, 512], F32, psum=True)
    pq = Ring(P, "pq", 2, [128, 512], F32, psum=True)
    qbf = Ring(P, "qbf", 2, [128, 512], BF16)
    t1r = Ring(P, "t1", 2, [128, 512], F32)
    t2r = Ring(P, "t2", 2, [128, 512], F32)
    obr = Ring(P, "ob", 3, [128, S], BF16)
    vbr = Ring(P, "vb", 2, [128, 512], BF16)

    pending = []

    def flush():
        while pending:
            pending.pop(0)()

    for slab in range(14):
        c0 = slab * 512
        ws, Tws = wst.next()
        wb, Twb = wbf.next()
        P.dma("sync", ws[:], W[:, c0:c0 + 512].rearrange("(kt p) c -> p kt c", p=128), writes=[Tws], st=Tws)
        P.op("gpsimd", lambda e, ws=ws, wb=wb: e.tensor_copy(out=wb[:], in_=ws[:]), reads=[Tws], writes=[Twb])
        if 4 <= slab < 6:
            for tt in range(NT):
                a, Ta = acc.next()
                for kt in range(8):
                    mm(P, a[:], C.hT[:, kt, tt * 128:(tt + 1) * 128], wb[:, kt, :], kt == 0, kt == 7,
                       [C.ThT[tt // 4], Twb], [Ta])
                flush()
                vb, Tvb = vbr.next()
                P.op("scalar", lambda e, vb=vb, a=a: e.copy(out=vb[:], in_=a[:]), reads=[Ta], writes=[Tvb])
                P.dma("gpsimd", C.V[tt * 128:(tt + 1) * 128, c0 - 2048:c0 - 2048 + 512], vb[:],
                      reads=[Tvb], st=Tvb)
            continue
        for dti in range(4):
            col = c0 + dti * 128
            ob, Tob = obr.next()
            for blk in range(4):
                a, Ta = acc.next()
                tb = slice(blk * 512, (blk + 1) * 512)
                for kt in range(8):
                    mm(P, a[:], wb[:, kt, dti * 128:(dti + 1) * 128], C.hT[:, kt, tb], kt == 0, kt == 7,
                       [C.ThT[blk], Twb], [Ta])
                flush()
                if col < 2048 and not C.norot:
                    qb, Tqb = qbf.next()
                    p2, Tp2 = pq.next()
                    t1, Tt1 = t1r.next()
                    t2, Tt2 = t2r.next()
                    P.op("scalar", lambda e, qb=qb, a=a: e.copy(out=qb[:], in_=a[:]), reads=[Ta], writes=[Tqb])
                    P.op("vector", lambda e, t1=t1, a=a, tb=tb: e.tensor_mul(out=t1[:], in0=a[:], in1=crot[:, tb]),
                         reads=[Ta, Tcr], writes=[Tt1])

                    def later(qb=qb, Tqb=Tqb, p2=p2, Tp2=Tp2, t1=t1, Tt1=Tt1, t2=t2, Tt2=Tt2, ob=ob, Tob=Tob, tb=tb):
                        mm(P, p2[:], C.Pm, qb[:], True, True, [Tqb, C.Tcbs], [Tp2])
                        P.op("vector", lambda e: e.tensor_mul(out=t2[:], in0=p2[:], in1=srot[:, tb]),
                             reads=[Tp2, Tsr], writes=[Tt2])
                        P.op("gpsimd", lambda e: e.tensor_add(out=ob[:, tb], in0=t1[:], in1=t2[:]),
                             reads=[Tt1, Tt2], writes=[Tob], acc=True)
                    pending.append(later)
                elif col < 4096 or C.nosig:
                    P.op("scalar", lambda e, ob=ob, a=a, tb=tb: e.copy(out=ob[:, tb], in_=a[:]),
                         reads=[Ta], writes=[Tob], acc=True)
                else:
                    P.op("scalar", lambda e, ob=ob, a=a, tb=tb: e.activation(out=ob[:, tb], in_=a[:], func=AF.Sigmoid),
                         reads=[Ta], writes=[Tob], acc=True)
            flush()
            if col < 1024:
                dst = C.QT[col // 128]
            elif col < 2048:
                dst = C.KT[(col - 1024) // 128]
            elif col < 3584:
                dst = C.ufT[(col - 3072) // 128]
            elif col < 4096:
                dst = C.usT[(col - 3584) // 128]
            else:
                dst = C.gT[(col - 4096) // 128]
            P.dma("gpsimd", dst, ob[:], reads=[Tob], st=Tob)
    P.end_phase()
    P.pop()


def load_cast(P, dst_bf, Tdst, src_ap, st_ring, eng="gpsimd"):
    ws, Tws = st_ring.next()
    shp = list(src_ap.shape)
    view = ws[:, 0:shp[1], 0:shp[2]]
    P.dma("sync", view, src_ap, writes=[Tws], st=Tws)
    P.op(eng, lambda e: e.tensor_copy(out=dst_bf, in_=view), reads=[Tws], writes=[Tdst], acc=True)


def load_cols(P, C, dst, Tdst, rows_ap, n, ps_ring, name):
    raw = P.sb("raw_" + name, [n, 128], F32)
    Traw = Tile()
    P.dma("sync", raw[:], rows_ap, writes=[Traw], st=Traw)
    ps, Tps = ps_ring.next()
    P.op("tensor", lambda e: e.transpose(ps[:, 0:n], raw[0:n, :], C.idf[0:n, 0:n]), reads=[Traw, C.Tidf], writes=[Tps])
    P.op("vector", lambda e: e.tensor_copy(out=dst, in_=ps[:, 0:n]), reads=[Tps], writes=[Tdst])


def phase_attn(P, C, l):
    lam_init = 0.8 - 0.6 * math.exp(-0.3 * l)
    P.push()
    lq = P.sb("lq", [128, 4, 64], F32)
    Tlq = Tile()
    for i, n in enumerate(["lambda_q1", "lambda_k1", "lambda_q2", "lambda_k2"]):
        P.dma("sync", lq[:, i, :], C.W[n][l].partition_broadcast(128), writes=[Tlq], st=Tlq)
    sc = P.sb("lsc", [128, 8], F32)
    Tsc = Tile()
    pr = P.sb("lpr", [128, 2, 64], F32)
    Tpr = Tile()
    P.op("vector", lambda e: e.tensor_mul(out=pr[:, 0, :], in0=lq[:, 0, :], in1=lq[:, 1, :]), reads=[Tlq], writes=[Tpr])
    P.op("vector", lambda e: e.tensor_mul(out=pr[:, 1, :], in0=lq[:, 2, :], in1=lq[:, 3, :]), reads=[Tlq], writes=[Tpr])
    P.op("vector", lambda e: e.reduce_sum(out=sc[:, 0:2], in_=pr[:], axis=mybir.AxisListType.X), reads=[Tpr], writes=[Tsc])
    P.op("scalar", lambda e: e.activation(out=sc[:, 2:4], in_=sc[:, 0:2], func=AF.Exp), reads=[Tsc], writes=[Tsc])
    P.op("vector", lambda e: e.tensor_sub(out=sc[:, 4:5], in0=sc[:, 3:4], in1=sc[:, 2:3]), reads=[Tsc], writes=[Tsc])
    P.op("vector", lambda e: e.tensor_scalar(out=sc[:, 4:5], in0=sc[:, 4:5], scalar1=-lam_init, scalar2=None, op0=ALU.add),
         reads=[Tsc], writes=[Tsc])
    P.dma("sync", sc[:, 5:6], C.W["attn_subln_g"][l].rearrange("(a b) -> a b", b=1), writes=[Tsc], st=Tsc)
    P.op("vector", lambda e: e.tensor_scalar(out=sc[:, 5:6], in0=sc[:, 5:6], scalar1=1.0 - lam_init, scalar2=None, op0=ALU.mult),
         reads=[Tsc], writes=[Tsc])
    neg_lam = sc[:, 4:5]
    gcol = sc[:, 5:6]

    qr = Ring(P, "aq", 2, [128, S], BF16)
    kr = Ring(P, "ak", 2, [128, S], BF16)
    vr = Ring(P, "av", 2, [128, NT, 128], BF16)
    sps = Ring(P, "sps", 2, [128, 512], F32, psum=True)
    ops_ = [Ring(P, "ops", 1, [128, 512], F32, psum=True) for _ in range(2)]
    zps = [Ring(P, "zps", 1, [128, 512], F32, psum=True) for _ in range(2)]
    ssp = Ring(P, "ssp", 1, [128, 512], F32, psum=True)
    er = Ring(P, "ae", 4, [128, 512], BF16)
    esr = Ring(P, "aes", 2, [128, 512], F32)
    ebr = Ring(P, "aeb", 2, [128, 512], BF16)
    r0r = Ring(P, "ar0", 2, [128, 512], F32)
    t0r = Ring(P, "at0", 2, [128, 512], F32)
    t1r = Ring(P, "at1", 2, [128, 512], F32)
    orr = Ring(P, "ao", 2, [128, 512], F32)
    sqr = Ring(P, "asq", 2, [128, 512], BF16)
    rsr = Ring(P, "ars", 2, [128, 512], F32)
    for h in range(8):
        q, Tq = qr.next()
        k, Tk = kr.next()
        v, Tv = vr.next()
        P.dma("sync", q[:], C.QT[h], writes=[Tq], st=Tq)
        P.dma("sync", k[:], C.KT[h], writes=[Tk], st=Tk)
        P.dma("sync", v[:], C.V[:, h * 128:(h + 1) * 128].rearrange("(tt p) e -> p tt e", p=128), writes=[Tv], st=Tv)
        for qb in range(4):
            qs = slice(qb * 512, (qb + 1) * 512)
            accs = []
            for c in range(2):
                o, To = ops_[c].next()
                zz, Tz = zps[c].next()
                accs.append((o, To, zz, Tz))
                prev = None
                for kt in range(NT):
                    sp, Tsp = sps.next()
                    mm(P, sp[:], k[c * 64:(c + 1) * 64, kt * 128:(kt + 1) * 128], q[c * 64:(c + 1) * 64, qs],
                       True, True, [Tq, Tk], [Tsp])
                    if prev is not None:
                        pe_, Tpe, pkt = prev
                        mm(P, o[:], v[:, pkt, :], pe_[:], pkt == 0, False, [Tv, Tpe], [To])
                    e_, Te = er.next()
                    P.op("scalar", lambda e, e_=e_, sp=sp: e.activation(out=e_[:], in_=sp[:], func=AF.Exp, scale=0.125),
                         reads=[Tsp], writes=[Te])
                    if kt == 0:
                        es, Tes = esr.next()
                        P.op("vector", lambda e, es=es, e_=e_: e.tensor_copy(out=es[:], in_=e_[:]), reads=[Te], writes=[Tes])
                    else:
                        P.op("vector", lambda e, es=es, e_=e_: e.tensor_add(out=es[:], in0=es[:], in1=e_[:]), reads=[Te, Tes], writes=[Tes])
                    prev = (e_, Te, kt)
                pe_, Tpe, pkt = prev
                mm(P, o[:], v[:, pkt, :], pe_[:], False, True, [Tv, Tpe], [To])
                eb, Teb = ebr.next()
                P.op("scalar", lambda e, eb=eb, es=es: e.copy(out=eb[:], in_=es[:]), reads=[Tes], writes=[Teb])
                mm(P, zz[:], C.ones, eb[:], True, True, [C.Tcbs, Teb], [Tz])
            (o0, To0, z0, Tz0), (o1, To1, z1, Tz1) = accs
            r0, Tr0 = r0r.next()
            t0, Tt0 = t0r.next()
            t1, Tt1 = t1r.next()
            oo, Too = orr.next()
            sq, Tsq = sqr.next()
            rs, Trs = rsr.next()
            ss, Tss = ssp.next()
            P.op("vector", lambda e, r0=r0, z0=z0: e.reciprocal(out=r0[:], in_=z0[:]), reads=[Tz0], writes=[Tr0])
            P.op("vector", lambda e, t0=t0, o0=o0, r0=r0: e.tensor_mul(out=t0[:], in0=o0[:], in1=r0[:]), reads=[To0, Tr0], writes=[Tt0])
            P.op("vector", lambda e, r0=r0, z1=z1: e.reciprocal(out=r0[:], in_=z1[:]), reads=[Tz1, Tt0], writes=[Tr0])
            P.op("vector", lambda e, t1=t1, o1=o1, r0=r0: e.tensor_mul(out=t1[:], in0=o1[:], in1=r0[:]), reads=[To1, Tr0], writes=[Tt1])
            P.op("vector", lambda e, oo=oo, t1=t1, t0=t0: e.scalar_tensor_tensor(out=oo[:], in0=t1[:], scalar=neg_lam, in1=t0[:],
                                                                             op0=ALU.mult, op1=ALU.add),
                 reads=[Tt1, Tt0, Tsc], writes=[Too])
            P.op("gpsimd", lambda e, sq=sq, oo=oo: e.tensor_mul(out=sq[:], in0=oo[:], in1=oo[:]), reads=[Too], writes=[Tsq])
            mm(P, ss[:], C.ones, sq[:], True, True, [C.Tcbs, Tsq], [Tss])
            P.op("vector", lambda e, rs=rs, ss=ss: e.tensor_scalar(out=rs[:], in0=ss[:], scalar1=1.0 / 128, scalar2=EPS,
                                                                   op0=ALU.mult, op1=ALU.add), reads=[Tss], writes=[Trs])
            P.op("scalar", lambda e, rs=rs: e.activation(out=rs[:], in_=rs[:], func=AF.Sqrt), reads=[Trs], writes=[Trs])
            P.op("vector", lambda e, rs=rs: e.reciprocal(out=rs[:], in_=rs[:]), reads=[Trs], writes=[Trs])
            P.op("vector", lambda e, oo=oo, rs=rs, h=h, qs=qs: e.scalar_tensor_tensor(out=C.oT[:, h, qs], in0=oo[:], scalar=gcol, in1=rs[:],
                                                                                   op0=ALU.mult, op1=ALU.mult),
                 reads=[Too, Trs, Tsc], writes=[C.ToT[h][qb]])
    P.end_phase("attn")
    P.pop()


def phase_tables(P, C):
    P.push()
    io = P.sb("io", [128, S], F32)
    Tio = Tile()
    P.op("gpsimd", lambda e: e.iota(io[:], [[1, S]], base=0, channel_multiplier=0, allow_small_or_imprecise_dtypes=True),
         writes=[Tio])
    ur = Ring(P, "tu", 2, [128, S], F32)
    kir = Ring(P, "tki", 2, [128, S], I32)
    kfr = Ring(P, "tkf", 2, [128, S], F32)
    fr = Ring(P, "tf", 2, [128, S], F32)
    gr = Ring(P, "tg", 2, [128, S], F32)
    f3r = Ring(P, "tf3", 2, [128, S], F32)
    sr = Ring(P, "tsin", 2, [128, S], BF16)
    cr = Ring(P, "tcos", 2, [128, S], BF16)
    scol = P.sb("scol", [128, 17], F32)
    Tscol = Tile()
    for st in range(17):
        P.op("vector", lambda e, st=st: e.tensor_scalar(out=scol[:, st:st + 1], in0=C.cfs[:, 1:2], scalar1=float(128 * st if st < 16 else 0),
                                                        scalar2=None, op0=ALU.add), reads=[C.Tcfs], writes=[Tscol], acc=True)
    for st in range(17):
        n = S if st < 16 else 128
        inv = 1.0 / (S if st < 16 else 128)
        u, Tu = ur.next()
        ki, Tki = kir.next()
        kf, Tkf = kfr.next()
        f, Tf = fr.next()
        g, Tg = gr.next()
        f3, Tf3 = f3r.next()
        sn, Tsn = sr.next()
        cs, Tcs = cr.next()
        P.op("vector", lambda e, u=u, st=st, n=n, inv=inv: e.tensor_scalar(out=u[:, :n], in0=io[:, :n], scalar1=scol[:, st:st + 1], scalar2=inv,
                                                                           op0=ALU.mult, op1=ALU.mult), reads=[Tio, Tscol], writes=[Tu])
        P.op("vector", lambda e, u=u, ki=ki, n=n: e.tensor_copy(out=ki[:, :n], in_=u[:, :n]), reads=[Tu], writes=[Tki])
        P.op("gpsimd", lambda e, kf=kf, ki=ki, n=n: e.tensor_copy(out=kf[:, :n], in_=ki[:, :n]), reads=[Tki], writes=[Tkf])
        P.op("gpsimd", lambda e, f=f, u=u, kf=kf, n=n: e.tensor_sub(out=f[:, :n], in0=u[:, :n], in1=kf[:, :n]), reads=[Tu, Tkf], writes=[Tf])
        sgn = 1.0 if st < 16 else -1.0
        P.op("scalar", lambda e, sn=sn, f=f, n=n, sgn=sgn: e.activation(out=sn[:, :n], in_=f[:, :n], func=AF.Sin, scale=SIN_SCALE * sgn),
             reads=[Tf], writes=[Tsn])
        P.op("vector", lambda e, g=g, f=f, n=n: e.tensor_scalar(out=g[:, :n], in0=f[:, :n], scalar1=0.25, scalar2=0.5, op0=ALU.add, op1=ALU.is_gt),
             reads=[Tf], writes=[Tg])
        P.op("vector", lambda e, f3=f3, f=f, g=g, n=n: e.scalar_tensor_tensor(out=f3[:, :n], in0=f[:, :n], scalar=0.25, in1=g[:, :n],
                                                                            op0=ALU.add, op1=ALU.subtract), reads=[Tf, Tg], writes=[Tf3])
        P.op("scalar", lambda e, cs=cs, f3=f3, n=n: e.activation(out=cs[:, :n], in_=f3[:, :n], func=AF.Sin, scale=SIN_SCALE),
             reads=[Tf3], writes=[Tcs])
        if st < 16:
            P.dma("sync", C.dftc[st * 128:(st + 1) * 128, :], cs[:], reads=[Tcs], st=Tcs)
            P.dma("sync", C.dfts[st * 128:(st + 1) * 128, :], sn[:], reads=[Tsn], st=Tsn)
        else:
            P.op("gpsimd", lambda e, cs=cs: e.tensor_copy(out=C.cc[:, 0:128], in_=cs[:, 0:128]), reads=[Tcs], writes=[C.Tcc])
            P.op("gpsimd", lambda e, sn=sn: e.tensor_copy(out=C.cc[:, 128:256], in_=sn[:, 0:128]), reads=[Tsn], writes=[C.Tcc])
    P.end_phase("tables")
    P.pop()


def phase_fourier(P, C):
    P.push()
    uf = P.sb("uf", [128, 4, S], BF16)
    Tuf = Tile()
    P.dma("sync", uf[:], C.ufT.rearrange("g p s -> p g s"), writes=[Tuf], st=Tuf)
    G = P.sb("G", [128, NT, 4, 256], BF16)
    TG = Tile()
    gps = Ring(P, "gps", 3, [128, 512], F32, psum=True)
    fps = Ring(P, "fps", 3, [128, 512], F32, psum=True)
    n = 0
    for tt in range(NT):
        for g in range(4):
            gp, Tgp = gps.next()
            mm(P, gp[:, 0:256], uf[:, g, tt * 128:(tt + 1) * 128], C.cc[:, :], True, True, [Tuf, C.Tcc], [Tgp])
            eng = "scalar" if n % 2 == 0 else "vector"
            n += 1
            if eng == "scalar":
                P.op("scalar", lambda e, gp=gp, tt=tt, g=g: e.copy(out=G[:, tt, g, :], in_=gp[:, 0:256]), reads=[Tgp], writes=[TG], acc=True)
            else:
                P.op("vector", lambda e, gp=gp, tt=tt, g=g: e.tensor_copy(out=G[:, tt, g, :], in_=gp[:, 0:256]), reads=[Tgp], writes=[TG], acc=True)
    tcr = Ring(P, "tabc", 1, [128, NT, 512], BF16)
    tsr = Ring(P, "tabs", 1, [128, NT, 512], BF16)
    for j in range(4):
        tc, Ttc = tcr.next()
        ts, Tts = tsr.next()
        P.dma("sync", tc[:], C.dftc[:, j * 512:(j + 1) * 512].rearrange("(st p) c -> p st c", p=128), writes=[Ttc], st=Ttc)
        P.dma("sync", ts[:], C.dfts[:, j * 512:(j + 1) * 512].rearrange("(st p) c -> p st c", p=128), writes=[Tts], st=Tts)
        for g in range(4):
            fp, Tfp = fps.next()
            for st in range(NT):
                mm(P, fp[:], G[:, st, g, 0:128], tc[:, st, :], st == 0, False, [TG, Ttc], [Tfp])
                mm(P, fp[:], G[:, st, g, 128:256], ts[:, st, :], False, st == NT - 1, [TG, Tts], [Tfp])
            P.op("scalar", lambda e, fp=fp, g=g, j=j: e.activation(out=C.fT[:, g, j * 512:(j + 1) * 512], in_=fp[:], func=AF.Copy, scale=1.0 / 512),
                 reads=[Tfp], writes=[C.TfT[g][j]])
    P.end_phase("fourier")
    P.pop()


_CACHE = {}


def kernel(**inputs):
    n = 8
    if "nc" not in _CACHE:
        _CACHE["nc"] = build()[0]
    nc = _CACHE["nc"]
    cb, cf = host_consts()
    in_maps = []
    for c in range(n):
        m = {"x": np.ascontiguousarray(inputs["x"][2 * c:2 * c + 2]),
             "p": np.ascontiguousarray(inputs["p"][:, 2 * c:2 * c + 2]),
             "pos": np.ascontiguousarray(inputs["positions"][2 * c:2 * c + 2]).astype(np.int32),
             "cb": cb, "cf": cf}
        for name, _ in WNAMES:
            m[name] = np.ascontiguousarray(inputs[name])
        in_maps.append(m)
    res = run_bass_kernel_spmd(nc, in_maps, core_ids=list(range(n)))
    return np.concatenate([np.asarray(r["y"]) for r in res.results], axis=0).astype(np.float32)


def phase_ssm(P, C, l):
    TC = 128
    NCH = S // TC
    W = C.W
    P.push()
    ctab = P.sb("ctab", [128, 32, TC + 1], F32)
    stab = P.sb("stab", [128, 32, TC + 1], F32)
    Ttab = Tile()
    rho = P.sb("rho", [128, 32], F32)
    Trho = Tile()
    Bl = P.sb("Bl", [128, 16, 2, 128], BF16)
    TBl = Tile()
    Cl = P.sb("Cl", [128, 2, 16, 2, 128], BF16)
    TCl = Tile()
    dcol = P.sb("dcol", [128, 4], F32)
    Tdcol = Tile()
    P.push()
    idf, Tidf = C.idf, C.Tidf
    araw = P.sb("araw", [32, 2, 128], F32)
    Taraw = Tile()
    P.dma("sync", araw[:, 0, :], W["ssm_a_re"][l].rearrange("d g p -> (d g p)").rearrange("(r q) -> r q", q=128), writes=[Taraw], st=Taraw)
    P.dma("sync", araw[:, 1, :], W["ssm_a_im"][l].rearrange("d g p -> (d g p)").rearrange("(r q) -> r q", q=128), writes=[Taraw], st=Taraw)
    pp = Ring(P, "spp", 2, [128, 512], F32, psum=True)
    load_cols(P, C, dcol[:], Tdcol, W["ssm_d"][l].rearrange("(t p) -> t p", p=128), 4, pp, "dcol")
    pb = Ring(P, "spb", 2, [128, 1024], BF16, psum=True)
    ps0, Tps0 = pp.next()
    for ri in range(2):
        P.op("tensor", lambda e, ri=ri: e.transpose(ps0[:, ri * 32:(ri + 1) * 32], araw[0:32, ri, :], idf[0:32, 0:32]),
             reads=[Taraw, Tidf], writes=[Tps0], acc=(ri > 0))
    atr = P.sb("atr", [128, 2, 32], F32)
    Tatr = Tile()
    P.op("vector", lambda e: e.tensor_copy(out=atr[:], in_=ps0[:, 0:64].rearrange("p (a b) -> p a b", a=2)), reads=[Tps0], writes=[Tatr])
    ldt = P.sb("ldt", [128, 64], F32)
    Tldt = Tile()
    P.dma("sync", ldt[:], W["ssm_log_dt"][l].rearrange("d g -> (d g)").partition_broadcast(128), writes=[Tldt], st=Tldt)
    dtm = P.sb("dtm", [128, 32], F32)
    Tdtm = Tile()
    for gl in range(2):
        P.op("vector", lambda e, gl=gl: e.tensor_copy(
            out=dtm[gl * 64:(gl + 1) * 64, :].rearrange("p (d j) -> p d j", d=2),
            in_=ldt[gl * 64:(gl + 1) * 64, :].rearrange("p (d j g) -> p d j g", d=2, g=2)[:, :, :, gl]),
            reads=[Tldt], writes=[Tdtm], acc=(gl > 0))
    P.op("scalar", lambda e: e.activation(out=dtm[:], in_=dtm[:], func=AF.Exp), reads=[Tdtm], writes=[Tdtm])
    sm = P.sb("sm", [128, 16, 32], F32)
    Tsm = Tile()
    th = sm[:, 0, :]
    ar = sm[:, 1, :]
    P.op("vector", lambda e: e.tensor_mul(out=ar, in0=atr[:, 0, :], in1=dtm[:]), reads=[Tatr, Tdtm], writes=[Tsm])
    P.op("vector", lambda e: e.scalar_tensor_tensor(out=th, in0=atr[:, 1, :], scalar=1.0 / TWO_PI, in1=dtm[:], op0=ALU.mult, op1=ALU.mult),
         reads=[Tatr, Tdtm], writes=[Tsm])
    P.op("scalar", lambda e: e.activation(out=rho[:], in_=ar, func=AF.Exp), reads=[Tsm], writes=[Trho])
    tio = P.sb("tio", [128, TC + 1], F32)
    Ttio = Tile()
    P.op("gpsimd", lambda e: e.iota(tio[:], [[1, TC + 1]], base=0, channel_multiplier=0, allow_small_or_imprecise_dtypes=True), writes=[Ttio])
    NW = 8 * (TC + 1)
    u = P.sb("su", [128, 8, TC + 1], F32)
    ki = P.sb("ski", [128, NW], I32)
    kf = P.sb("skf", [128, NW], F32)
    f = P.sb("sf", [128, NW], F32)
    g = P.sb("sg", [128, NW], F32)
    f3 = P.sb("sf3", [128, NW], F32)
    Tu, Tki, Tkf, Tf, Tg, Tf3 = (Tile() for _ in range(6))
    for d8 in range(4):
        for j in range(8):
            dj = d8 * 8 + j
            P.op("vector", lambda e, j=j, dj=dj: e.tensor_scalar(out=u[:, j, :], in0=tio[:], scalar1=sm[:, 0, dj:dj + 1], scalar2=None, op0=ALU.mult),
                 reads=[Ttio, Tsm], writes=[Tu], acc=(j > 0))
        uf_ = u[:].rearrange("p a b -> p (a b)")
        P.op("vector", lambda e: e.tensor_copy(out=ki[:], in_=uf_), reads=[Tu], writes=[Tki])
        P.op("gpsimd", lambda e: e.tensor_copy(out=kf[:], in_=ki[:]), reads=[Tki], writes=[Tkf])
        P.op("gpsimd", lambda e: e.tensor_sub(out=f[:], in0=uf_, in1=kf[:]), reads=[Tu, Tkf], writes=[Tf])
        so = stab[:, d8 * 8:(d8 + 1) * 8, :].rearrange("p a b -> p (a b)")
        co = ctab[:, d8 * 8:(d8 + 1) * 8, :].rearrange("p a b -> p (a b)")
        P.op("scalar", lambda e, so=so: e.activation(out=so, in_=f[:], func=AF.Sin, scale=SIN_SCALE), reads=[Tf], writes=[Ttab], acc=True)
        P.op("vector", lambda e: e.tensor_scalar(out=g[:], in0=f[:], scalar1=0.25, scalar2=0.5, op0=ALU.add, op1=ALU.is_gt), reads=[Tf], writes=[Tg])
        P.op("vector", lambda e: e.scalar_tensor_tensor(out=f3[:], in0=f[:], scalar=0.25, in1=g[:], op0=ALU.add, op1=ALU.subtract),
             reads=[Tf, Tg], writes=[Tf3])
        P.op("scalar", lambda e, co=co: e.activation(out=co, in_=f3[:], func=AF.Sin, scale=SIN_SCALE), reads=[Tf3], writes=[Ttab], acc=True)
    c1 = ctab[:, :, 1]
    s1 = stab[:, :, 1]
    nr, ni, m2, t1, t2 = sm[:, 2, :], sm[:, 3, :], sm[:, 4, :], sm[:, 5, :], sm[:, 6, :]
    bsre, bsim, nbsim = sm[:, 8, :], sm[:, 9, :], sm[:, 10, :]
    lr, li = atr[:, 0, :], atr[:, 1, :]
    V = lambda fn, rd=(), wr=(Tsm,): P.op("vector", fn, reads=[Tsm, Ttab, Trho, Tatr] + list(rd), writes=list(wr))
    V(lambda e: e.tensor_mul(out=nr, in0=rho[:], in1=c1))
    V(lambda e: e.tensor_scalar(out=nr, in0=nr, scalar1=-1.0, scalar2=None, op0=ALU.add))
    V(lambda e: e.tensor_mul(out=ni, in0=rho[:], in1=s1))
    V(lambda e: e.tensor_mul(out=m2, in0=lr, in1=lr))
    V(lambda e: e.tensor_mul(out=t1, in0=li, in1=li))
    V(lambda e: e.tensor_add(out=m2, in0=m2, in1=t1))
    V(lambda e: e.reciprocal(out=m2, in_=m2))
    V(lambda e: e.tensor_mul(out=t1, in0=nr, in1=lr))
    V(lambda e: e.tensor_mul(out=t2, in0=ni, in1=li))
    V(lambda e: e.tensor_add(out=t1, in0=t1, in1=t2))
    V(lambda e: e.tensor_mul(out=bsre, in0=t1, in1=m2))
    V(lambda e: e.tensor_mul(out=t1, in0=ni, in1=lr))
    V(lambda e: e.tensor_mul(out=t2, in0=nr, in1=li))
    V(lambda e: e.tensor_sub(out=t1, in0=t1, in1=t2))
    V(lambda e: e.tensor_mul(out=bsim, in0=t1, in1=m2))
    V(lambda e: e.tensor_scalar(out=nbsim, in0=bsim, scalar1=-1.0, scalar2=None, op0=ALU.mult))
    Bnat = P.sb("Bnat", [128, 2, 16, 16], F32)
    TBn = Tile()
    P.dma("sync", Bnat[:, 0, :, :], W["ssm_b_re"][l].rearrange("g p h -> (g p h)").rearrange("(j q h) -> q j h", j=16, q=128), writes=[TBn], st=TBn)
    P.dma("sync", Bnat[:, 1, :, :], W["ssm_b_im"][l].rearrange("g p h -> (g p h)").rearrange("(j q h) -> q j h", j=16, q=128), writes=[TBn], st=TBn)
    XB = P.sb("XB", [128, 16, 2, 128], BF16)
    TXB = Tile()
    P.op("gpsimd", lambda e: e.memset(XB[:], 0.0), writes=[TXB])
    for j in range(16):
        for ri in range(2):
            for gl in range(2):
                c0 = 32 * (j % 4) + 16 * gl
                P.op("vector",
                     lambda e, j=j, ri=ri, gl=gl, c0=c0: e.tensor_copy(out=XB[gl * 64:(gl + 1) * 64, j, ri, c0:c0 + 16],
                                                                      in_=Bnat[gl * 64:(gl + 1) * 64, ri, j, :]),
                     reads=[TBn], writes=[TXB], acc=True)
    for grp in range(4):
        pbt, Tpbt = pb.next()
        for i in range(8):
            idx = grp * 8 + i
            j, ri = idx // 2, idx % 2
            P.op("tensor", lambda e, i=i, j=j, ri=ri, pbt=pbt: e.transpose(pbt[:, i * 128:(i + 1) * 128], XB[:, j, ri, :], C.ident),
                 reads=[TXB, C.Tcbs], writes=[Tpbt], acc=(i > 0))
        P.op("scalar", lambda e, grp=grp, pbt=pbt: e.copy(out=Bl[:, grp * 4:(grp + 1) * 4, :, :].rearrange("p a b c -> p (a b c)"), in_=pbt[:]),
             reads=[Tpbt], writes=[TBl], acc=True)
    Craw = P.sb("Craw", [128, 2, 4, 64], F32)
    TCr = Tile()
    P.dma("sync", Craw[:, 0, :, :], W["ssm_c_re"][l].rearrange("g h p -> (g h p)").rearrange("(t m p) -> m t p", t=4, m=128), writes=[TCr], st=TCr)
    P.dma("sync", Craw[:, 1, :, :], W["ssm_c_im"][l].rearrange("g h p -> (g h p)").rearrange("(t m p) -> m t p", t=4, m=128), writes=[TCr], st=TCr)
    xcr = Ring(P, "XC", 2, [128, 2, 128], F32)
    tmpr = Ring(P, "ctmp", 2, [128, 128], F32)
    n = 0
    for j in range(16):
        XC, TXC = xcr.next()
        for ri in range(2):
            for gl in range(2):
                mc = 2 + (j % 4) * 2 + gl
                P.op("vector" if n % 2 == 0 else "gpsimd",
                     lambda e, XC=XC, j=j, ri=ri, gl=gl, mc=mc: e.tensor_scalar(out=XC[:, ri, gl * 64:(gl + 1) * 64], in0=Craw[:, ri, j // 4, :],
                                                                               scalar1=C.cfs[:, mc:mc + 1], scalar2=None, op0=ALU.mult),
                     reads=[TCr, C.Tcfs], writes=[TXC], acc=False)
                n += 1
        pt, Tpt = pp.next()
        for ri in range(2):
            P.op("tensor", lambda e, XC=XC, ri=ri, pt=pt: e.transpose(pt[:, ri * 128:(ri + 1) * 128], XC[:, ri, :], idf[:]),
                 reads=[TXC, Tidf], writes=[Tpt], acc=(ri > 0))
        tre, tim = pt[:, 0:128], pt[:, 128:256]
        for d in range(2):
            dj = d * 16 + j
            tm, Ttm = tmpr.next()
            P.op("vector", lambda e, tm=tm, tim=tim, dj=dj: e.tensor_scalar(out=tm[:], in0=tim, scalar1=sm[:, 9, dj:dj + 1], scalar2=None, op0=ALU.mult),
                 reads=[Tpt, Tsm], writes=[Ttm])
            P.op("vector", lambda e, tm=tm, tre=tre, d=d, j=j, dj=dj: e.scalar_tensor_tensor(out=Cl[:, d, j, 0, :], in0=tre, scalar=sm[:, 8, dj:dj + 1], in1=tm[:],
                                                                                          op0=ALU.mult, op1=ALU.subtract),
                 reads=[Tpt, Tsm, Ttm], writes=[TCl], acc=True)
            tm2, Ttm2 = tmpr.next()
            P.op("vector", lambda e, tm2=tm2, tim=tim, dj=dj: e.tensor_scalar(out=tm2[:], in0=tim, scalar1=sm[:, 8, dj:dj + 1], scalar2=None, op0=ALU.mult),
                 reads=[Tpt, Tsm], writes=[Ttm2])
            P.op("vector", lambda e, tm2=tm2, tre=tre, d=d, j=j, dj=dj: e.scalar_tensor_tensor(out=Cl[:, d, j, 1, :], in0=tre, scalar=sm[:, 10, dj:dj + 1], in1=tm2[:],
                                                                                            op0=ALU.mult, op1=ALU.subtract),
                 reads=[Tpt, Tsm, Ttm2], writes=[TCl], acc=True)
    P.end_phase("ssm_prep")
    P.pop()

    yacc = C.hT[:].rearrange("p k s -> p (k s)").bitcast(F32).rearrange("p (t s) -> p t s", t=4)
    Tyacc = [Tile() for _ in range(4)]
    car = P.sb("car", [128, 2, 2, 16], F32)
    Tcar = [[Tile() for _ in range(4)] for _ in range(2)]
    P.op("gpsimd", lambda e: e.memset(car[:], 0.0), writes=[t for r in Tcar for t in r])
    ucr = Ring(P, "uc", 3, [128, 4, TC], BF16)
    rrer = Ring(P, "rre", 2, [128, 512], F32, psum=True)
    rimr = Ring(P, "rim", 2, [128, 512], F32, psum=True)
    ypr = Ring(P, "yp", 2, [128, 512], F32, psum=True)
    a_r = [Ring(P, "sa%d" % i, 1, [128, 4, TC], F32) for i in range(4)]
    m_r = [Ring(P, "smm%d" % i, 2, [128, 4, TC], F32) for i in range(2)]
    w_r = [Ring(P, "sw%d" % i, 2, [128, 4, TC], F32) for i in range(2)]
    b_r = [Ring(P, "sb%d" % i, 1, [128, 4, TC], F32) for i in range(4)]
    x_r = [Ring(P, "sx%d" % i, 2, [128, 4, TC], BF16) for i in range(2)]
    cp_r = Ring(P, "scp", 2, [128, 4, 4], F32)
    ys_r = Ring(P, "sys", 2, [128, TC], F32)
    ys2_r = Ring(P, "sys2", 2, [128, TC], F32)
    pending = []

    def flush():
        while pending:
            pending.pop(0)()

    for d in range(2):
        for n in range(NCH):
            ci = n if d == 0 else NCH - 1 - n
            t0 = ci * TC
            uc, Tuc = ucr.next()
            P.dma("sync", uc[:], C.usT[:, :, t0:t0 + TC].rearrange("g p s -> p g s"), writes=[Tuc], st=Tuc)
            for t in range(4):
                rre, Trre = rrer.next()
                rim, Trim = rimr.next()
                yp, Typ = ypr.next()
                rhs = uc[:, t, :] if d == 0 else uc[:, t, ::-1]
                for jj in range(4):
                    j = 4 * t + jj
                    mm(P, rre[:, jj * TC:(jj + 1) * TC], Bl[:, j, 0, :], rhs, True, True, [TBl, Tuc], [Trre])
                    mm(P, rim[:, jj * TC:(jj + 1) * TC], Bl[:, j, 1, :], rhs, True, True, [TBl, Tuc], [Trim])
                flush()
                dj0 = d * 16 + 4 * t
                c4 = ctab[:, dj0:dj0 + 4, 0:TC]
                s4 = stab[:, dj0:dj0 + 4, 0:TC]
                rre3 = rre[:].rearrange("p (a b) -> p a b", a=4)
                rim3 = rim[:].rearrange("p (a b) -> p a b", a=4)
                (a1, Ta1), (a2, Ta2), (a3, Ta3), (a4, Ta4) = [r.next() for r in a_r]
                P.op("vector", lambda e, a1=a1, rre3=rre3, c4=c4: e.tensor_mul(out=a1[:], in0=rre3, in1=c4), reads=[Trre, Ttab], writes=[Ta1])
                P.op("vector", lambda e, a2=a2, rim3=rim3, s4=s4: e.tensor_mul(out=a2[:], in0=rim3, in1=s4), reads=[Trim, Ttab], writes=[Ta2])
                P.op("vector", lambda e, a3=a3, rim3=rim3, c4=c4: e.tensor_mul(out=a3[:], in0=rim3, in1=c4), reads=[Trim, Ttab], writes=[Ta3])
                P.op("vector", lambda e, a4=a4, rre3=rre3, s4=s4: e.tensor_mul(out=a4[:], in0=rre3, in1=s4), reads=[Trre, Ttab], writes=[Ta4])
                (mre, Tmre), (mim, Tmim) = [r.next() for r in m_r]
                P.op("vector", lambda e, mre=mre, a1=a1, a2=a2: e.tensor_add(out=mre[:], in0=a1[:], in1=a2[:]), reads=[Ta1, Ta2], writes=[Tmre])
                P.op("vector", lambda e, mim=mim, a3=a3, a4=a4: e.tensor_sub(out=mim[:], in0=a3[:], in1=a4[:]), reads=[Ta3, Ta4], writes=[Tmim])
                (wre, Twre), (wim, Twim) = [r.next() for r in w_r]
                Tc_ = Tcar[d][t]
                for jj in range(4):
                    j = 4 * t + jj
                    dj = d * 16 + j
                    P.op("vector", lambda e, wre=wre, mre=mre, jj=jj, j=j, dj=dj, d=d: e.tensor_tensor_scan(
                        out=wre[:, jj, :], data0=rho[:, dj:dj + 1].to_broadcast([128, TC]), data1=mre[:, jj, :],
                        initial=car[:, d, 0, j:j + 1], op0=ALU.mult, op1=ALU.add), reads=[Tmre, Trho, Tc_], writes=[Twre], acc=(jj > 0))
                    P.op("vector", lambda e, wim=wim, mim=mim, jj=jj, j=j, dj=dj, d=d: e.tensor_tensor_scan(
                        out=wim[:, jj, :], data0=rho[:, dj:dj + 1].to_broadcast([128, TC]), data1=mim[:, jj, :],
                        initial=car[:, d, 1, j:j + 1], op0=ALU.mult, op1=ALU.add), reads=[Tmim, Trho, Tc_], writes=[Twim], acc=(jj > 0))
                cT = ctab[:, dj0:dj0 + 4, TC]
                sT = stab[:, dj0:dj0 + 4, TC]
                wlr = wre[:, :, TC - 1]
                wli = wim[:, :, TC - 1]
                cp, Tcp = cp_r.next()
                G_ = lambda fn, rd, wr, acc=False: P.op("gpsimd", fn, reads=rd, writes=wr, acc=acc)
                G_(lambda e, cp=cp, wlr=wlr, cT=cT: e.tensor_mul(out=cp[:, 0, :], in0=wlr, in1=cT), [Twre, Ttab], [Tcp])
                G_(lambda e, cp=cp, wli=wli, sT=sT: e.tensor_mul(out=cp[:, 1, :], in0=wli, in1=sT), [Twim, Ttab], [Tcp], True)
                G_(lambda e, cp=cp, wlr=wlr, sT=sT: e.tensor_mul(out=cp[:, 2, :], in0=wlr, in1=sT), [Twre, Ttab], [Tcp], True)
                G_(lambda e, cp=cp, wli=wli, cT=cT: e.tensor_mul(out=cp[:, 3, :], in0=wli, in1=cT), [Twim, Ttab], [Tcp], True)
                G_(lambda e, cp=cp, d=d, t=t: e.tensor_sub(out=car[:, d, 0, 4 * t:4 * t + 4], in0=cp[:, 0, :], in1=cp[:, 1, :]), [Tcp], [Tc_])
                G_(lambda e, cp=cp, d=d, t=t: e.tensor_add(out=car[:, d, 1, 4 * t:4 * t + 4], in0=cp[:, 2, :], in1=cp[:, 3, :]), [Tcp], [Tc_], True)
                (b1, Tb1), (b2, Tb2), (b3, Tb3), (b4, Tb4) = [r.next() for r in b_r]
                (xre, Txre), (xim, Txim) = [r.next() for r in x_r]
                P.op("vector", lambda e, b1=b1, wre=wre, c4=c4: e.tensor_mul(out=b1[:], in0=wre[:], in1=c4), reads=[Twre, Ttab], writes=[Tb1])
                P.op("vector", lambda e, b2=b2, wim=wim, s4=s4: e.tensor_mul(out=b2[:], in0=wim[:], in1=s4), reads=[Twim, Ttab], writes=[Tb2])
                G_(lambda e, xre=xre, b1=b1, b2=b2: e.tensor_sub(out=xre[:], in0=b1[:], in1=b2[:]), [Tb1, Tb2], [Txre])
                G_(lambda e, b3=b3, wre=wre, s4=s4: e.tensor_mul(out=b3[:], in0=wre[:], in1=s4), [Twre, Ttab], [Tb3])
                G_(lambda e, b4=b4, wim=wim, c4=c4: e.tensor_mul(out=b4[:], in0=wim[:], in1=c4), [Twim, Ttab], [Tb4])
                G_(lambda e, xim=xim, b3=b3, b4=b4: e.tensor_add(out=xim[:], in0=b3[:], in1=b4[:]), [Tb3, Tb4], [Txim])

                def later(d=d, t=t, t0=t0, yp=yp, Typ=Typ, xre=xre, Txre=Txre, xim=xim, Txim=Txim, uc=uc, Tuc=Tuc):
                    for jj in range(4):
                        j = 4 * t + jj
                        mm(P, yp[:, 0:TC], Cl[:, d, j, 0, :], xre[:, jj, :], jj == 0, False, [TCl, Txre], [Typ])
                        mm(P, yp[:, 0:TC], Cl[:, d, j, 1, :], xim[:, jj, :], False, jj == 3, [TCl, Txim], [Typ])
                    if d == 0:
                        P.op("scalar", lambda e: e.copy(out=yacc[:, t, t0:t0 + TC], in_=yp[:, 0:TC]), reads=[Typ], writes=[Tyacc[t]], acc=True)
                    else:
                        ys, Tys = ys_r.next()
                        ys2, Tys2 = ys2_r.next()
                        P.op("vector", lambda e: e.tensor_add(out=ys[:], in0=yp[:, 0:TC], in1=yacc[:, t, t0:t0 + TC][:, ::-1]),
                             reads=[Typ, Tyacc[t]], writes=[Tys])
                        P.op("vector", lambda e: e.scalar_tensor_tensor(out=ys2[:], in0=uc[:, t, ::-1], scalar=dcol[:, t:t + 1], in1=ys[:],
                                                                        op0=ALU.mult, op1=ALU.add), reads=[Tuc, Tdcol, Tys], writes=[Tys2])
                        P.op("scalar", lambda e: e.activation(out=C.sT[:, t, t0:t0 + TC][:, ::-1], in_=ys2[:], func=AF.Gelu_apprx_tanh),
                             reads=[Tys2], writes=[C.TsT[t][t0 // 512]], acc=True)
                pending.append(later)
    flush()
    P.end_phase("ssm")
    P.pop()


def phase_merge(P, C, l):
    W = C.W
    P.push()
    bcol = P.sb("bcol", [128, 16], F32)
    Tbcol = Tile()
    wst = Ring(P, "mws", 2, [128, 20, 128], F32)
    wbr = Ring(P, "mwb", 2, [128, 20, 128], BF16)
    gr = Ring(P, "mg", 2, [128, 3, S], BF16)
    psr = Ring(P, "mps", 6, [128, 512], F32, psum=True)
    load_cols(P, C, bcol[:], Tbcol, W["b_glu"][l].rearrange("(c p) -> c p", p=128), 16, psr, "bcol")
    tar = Ring(P, "mta", 2, [128, 512], F32)
    tbr = Ring(P, "mtb", 2, [128, 512], F32)
    sgr = Ring(P, "msg", 2, [128, 512], F32)
    ycr = Ring(P, "myc", 2, [128, 512], F32)
    for i in range(8):
        ws, Tws = wst.next()
        wb, Twb = wbr.next()
        cs = slice(i * 128, (i + 1) * 128)
        P.dma("sync", ws[:, 0:8, :], W["w_attn_br"][l][:, cs].rearrange("(kt p) c -> p kt c", p=128), writes=[Tws], st=Tws)
        P.dma("sync", ws[:, 8:12, :], W["w_fourier_br"][l][:, cs].rearrange("(kt p) c -> p kt c", p=128), writes=[Tws], st=Tws)
        P.dma("sync", ws[:, 12:16, :], W["w_glu"][l][:, cs].rearrange("(kt p) c -> p kt c", p=128), writes=[Tws], st=Tws)
        P.dma("sync", ws[:, 16:20, :], W["w_glu"][l][:, 1024 + i * 128:1024 + (i + 1) * 128].rearrange("(kt p) c -> p kt c", p=128),
              writes=[Tws], st=Tws)
        P.op("gpsimd", lambda e, ws=ws, wb=wb: e.tensor_copy(out=wb[:], in_=ws[:]), reads=[Tws], writes=[Twb])
        g, Tg = gr.next()
        for a in range(3):
            P.dma("sync", g[:, a, :], C.gT[a * 8 + i], writes=[Tg], st=Tg)
        for blk in range(4):
            tb = slice(blk * 512, (blk + 1) * 512)
            ya, Tya = psr.next()
            yb, Tyb = psr.next()
            z1, Tz1 = psr.next()
            z2, Tz2 = psr.next()
            for kt in range(8):
                mm(P, ya[:], wb[:, kt, :], C.oT[:, kt, tb], kt == 0, kt == 7, [Twb, C.ToT[kt][blk]], [Tya])
            for kt in range(4):
                mm(P, yb[:], wb[:, 8 + kt, :], C.fT[:, kt, tb], kt == 0, kt == 3, [Twb, C.TfT[kt][blk]], [Tyb])
            for kt in range(4):
                mm(P, z1[:], wb[:, 12 + kt, :], C.sT[:, kt, tb], kt == 0, kt == 3, [Twb, C.TsT[kt][blk]], [Tz1])
            for kt in range(4):
                mm(P, z2[:], wb[:, 16 + kt, :], C.sT[:, kt, tb], kt == 0, kt == 3, [Twb, C.TsT[kt][blk]], [Tz2])
            ta, Tta = tar.next()
            t2, Tt2 = tbr.next()
            sg, Tsg = sgr.next()
            yc, Tyc = ycr.next()
            P.op("vector", lambda e, ta=ta, ya=ya, g=g, tb=tb: e.tensor_mul(out=ta[:], in0=ya[:], in1=g[:, 0, tb]), reads=[Tya, Tg], writes=[Tta])
            P.op("vector", lambda e, t2=t2, yb=yb, g=g, tb=tb: e.tensor_mul(out=t2[:], in0=yb[:], in1=g[:, 1, tb]), reads=[Tyb, Tg], writes=[Tt2])
            P.op("scalar", lambda e, sg=sg, z2=z2, i=i: e.activation(out=sg[:], in_=z2[:], func=AF.Sigmoid, bias=bcol[:, 8 + i:9 + i]),
                 reads=[Tz2, Tbcol], writes=[Tsg])
            P.op("vector", lambda e, yc=yc, z1=z1, sg=sg, i=i: e.scalar_tensor_tensor(out=yc[:], in0=z1[:], scalar=bcol[:, i:i + 1], in1=sg[:],
                                                                                  op0=ALU.add, op1=ALU.mult), reads=[Tz1, Tsg, Tbcol], writes=[Tyc])
            P.op("gpsimd", lambda e, ta=ta, t2=t2: e.tensor_add(out=ta[:], in0=ta[:], in1=t2[:]), reads=[Tta, Tt2], writes=[Tta])
            P.op("gpsimd", lambda e, yc=yc, g=g, tb=tb: e.tensor_mul(out=yc[:], in0=yc[:], in1=g[:, 2, tb]), reads=[Tyc, Tg], writes=[Tyc])
            P.op("gpsimd", lambda e, ta=ta, yc=yc, i=i, tb=tb: e.tensor_add(out=C.hT[:, i, tb], in0=ta[:], in1=yc[:]), reads=[Tta, Tyc],
                 writes=[C.ThT[blk]], acc=True)
    P.end_phase("merge")
    P.pop()


def load_weight_bf(P, dst, Tdst, src2d, nk, ncol, st_ring, kchunk=8):
    for k0 in range(0, nk, kchunk):
        k1 = min(nk, k0 + kchunk)
        ws, Tws = st_ring.next()
        view = ws[:, 0:k1 - k0, 0:ncol]
        P.dma("sync", view, src2d[k0 * 128:k1 * 128, :].rearrange("(kt p) c -> p kt c", p=128), writes=[Tws], st=Tws)
        P.op("gpsimd", lambda e, view=view, k0=k0, k1=k1: e.tensor_copy(out=dst[:, k0:k1, :], in_=view), reads=[Tws], writes=[Tdst], acc=True)


def phase_wout(P, C, l, b, xsrc):
    P.push()
    wo = P.sb("wo", [128, 8, D], BF16)
    Two = Tile()
    st = Ring(P, "wos", 2, [128, 8, 512], F32)
    for cb in range(2):
        ws, Tws = st.next()
        P.dma("sync", ws[:], C.W["w_out"][l][:, cb * 512:(cb + 1) * 512].rearrange("(kt p) c -> p kt c", p=128), writes=[Tws], st=Tws)
        P.op("gpsimd", lambda e, ws=ws, cb=cb: e.tensor_copy(out=wo[:, :, cb * 512:(cb + 1) * 512], in_=ws[:]), reads=[Tws], writes=[Two], acc=True)
    xr = Ring(P, "wx", 2, [128, D], F32)
    xnr = Ring(P, "wxn", 2, [128, D], F32)
    psr = Ring(P, "wps", 4, [128, 512], F32, psum=True)
    for tt in range(NT):
        xt, Tx = xr.next()
        xn, Txn = xnr.next()
        ts = slice(tt * 128, (tt + 1) * 128)
        P.dma("sync", xt[:], xsrc[ts, :], writes=[Tx], st=Tx)
        for cb in range(2):
            cs = slice(cb * 512, (cb + 1) * 512)
            ps, Tps = psr.next()
            for kt in range(8):
                mm(P, ps[:], C.hT[:, kt, ts], wo[:, kt, cs], kt == 0, kt == 7, [C.ThT[tt // 4], Two], [Tps])
            P.op("vector", lambda e, xn=xn, ps=ps, xt=xt, cs=cs: e.tensor_add(out=xn[:, cs], in0=ps[:], in1=xt[:, cs]), reads=[Tps, Tx], writes=[Txn],
                 acc=(cb > 0))
        P.dma("gpsimd", C.xres[b][ts, :], xn[:], reads=[Txn], st=Txn)
    P.end_phase("wout")
    P.pop()


def phase_ffn(P, C, l, b):
    W = C.W
    P.push()
    aT = P.sb("aT", [128, 22, S], BF16)
    TaT = [Tile() for _ in range(4)]
    P.push()
    cw = P.sb("cw", [128, 4, 22], F32)
    Tcw = Tile()
    craw = P.sb("craw", [88, 128], F32)
    Tcraw = Tile()
    for k in range(3):
        P.dma("sync", craw[k * 22:(k + 1) * 22, :], W["conv_w"][l][k].rearrange("(i p) -> i p", p=128), writes=[Tcraw], st=Tcraw)
    P.dma("sync", craw[66:88, :], W["conv_b"][l].rearrange("(i p) -> i p", p=128), writes=[Tcraw], st=Tcraw)
    cps = Ring(P, "cwps", 1, [128, 512], F32, psum=True)
    cp_, Tcp_ = cps.next()
    P.op("tensor", lambda e: e.transpose(cp_[:, 0:88], craw[0:88, :], C.idf[0:88, 0:88]), reads=[Tcraw, C.Tidf], writes=[Tcp_])
    P.op("vector", lambda e: e.tensor_copy(out=cw[:].rearrange("p a b -> p (a b)"), in_=cp_[:, 0:88]), reads=[Tcp_], writes=[Tcw])
    wst = Ring(P, "fws", 2, [128, 16, 128], F32)
    wbr = Ring(P, "fwb", 2, [128, 16, 128], BF16)
    gbr = Ring(P, "fgb", 2, [128, S + 2], F32)
    for gb_, Tgb_ in gbr.bufs:
        P.op("gpsimd", lambda e, gb_=gb_: e.memset(gb_[:], 0.0), writes=[Tgb_])
    vbr = Ring(P, "fvb", 2, [128, S], BF16)
    cbr = Ring(P, "fcb", 2, [128, S], F32)
    glr = Ring(P, "fgl", 1, [128, S], F32)
    psr = Ring(P, "fps", 6, [128, 512], F32, psum=True)
    for i in range(22):
        ws, Tws = wst.next()
        wb, Twb = wbr.next()
        P.dma("sync", ws[:, 0:8, :], W["w_up"][l][:, i * 128:(i + 1) * 128].rearrange("(kt p) c -> p kt c", p=128), writes=[Tws], st=Tws)
        P.dma("sync", ws[:, 8:16, :], W["w_up"][l][:, 2816 + i * 128:2816 + (i + 1) * 128].rearrange("(kt p) c -> p kt c", p=128),
              writes=[Tws], st=Tws)
        P.op("gpsimd", lambda e, ws=ws, wb=wb: e.tensor_copy(out=wb[:], in_=ws[:]), reads=[Tws], writes=[Twb])
        gb, Tgb = gbr.next()
        vb, Tvb = vbr.next()
        for blk in range(4):
            tb = slice(blk * 512, (blk + 1) * 512)
            pg, Tpg = psr.next()
            pv, Tpv = psr.next()
            for kt in range(8):
                mm(P, pg[:], wb[:, kt, :], C.hT[:, kt, tb], kt == 0, kt == 7, [Twb, C.ThT[blk]], [Tpg])
            for kt in range(8):
                mm(P, pv[:], wb[:, 8 + kt, :], C.hT[:, kt, tb], kt == 0, kt == 7, [Twb, C.ThT[blk]], [Tpv])
            P.op("scalar", lambda e, gb=gb, pg=pg, blk=blk: e.copy(out=gb[:, 1 + blk * 512:1 + (blk + 1) * 512], in_=pg[:]), reads=[Tpg], writes=[Tgb],
                 acc=(blk > 0))
            P.op("vector", lambda e, vb=vb, pv=pv, tb=tb: e.tensor_copy(out=vb[:, tb], in_=pv[:]), reads=[Tpv], writes=[Tvb], acc=(blk > 0))
        c1, Tc1 = cbr.next()
        c2, Tc2 = cbr.next()
        gl, Tgl = glr.next()
        P.op("vector", lambda e, c1=c1, gb=gb, i=i: e.tensor_scalar(out=c1[:], in0=gb[:, 1:S + 1], scalar1=cw[:, 1, i:i + 1], scalar2=cw[:, 3, i:i + 1],
                                                                   op0=ALU.mult, op1=ALU.add), reads=[Tgb, Tcw], writes=[Tc1])
        P.op("vector", lambda e, c2=c2, c1=c1, gb=gb, i=i: e.scalar_tensor_tensor(out=c2[:], in0=gb[:, 0:S], scalar=cw[:, 0, i:i + 1], in1=c1[:],
                                                                              op0=ALU.mult, op1=ALU.add), reads=[Tgb, Tcw, Tc1], writes=[Tc2])
        P.op("vector", lambda e, c2=c2, c1=c1, gb=gb, i=i: e.scalar_tensor_tensor(out=c1[:], in0=gb[:, 2:S + 2], scalar=cw[:, 2, i:i + 1], in1=c2[:],
                                                                              op0=ALU.mult, op1=ALU.add), reads=[Tgb, Tcw, Tc2], writes=[Tc1])
        P.op("scalar", lambda e, gl=gl, c1=c1: e.activation(out=gl[:], in_=c1[:], func=AF.Gelu_apprx_tanh), reads=[Tc1], writes=[Tgl])
        P.op("gpsimd", lambda e, gl=gl, vb=vb, i=i: e.tensor_mul(out=aT[:, i, :], in0=gl[:], in1=vb[:]), reads=[Tgl, Tvb], writes=TaT, acc=True)
    P.end_phase("ffn_up")
    P.pop()
    P.push()
    wd = P.sb("wd", [128, 22, 512], BF16)
    Twd = Tile()
    st = Ring(P, "fds", 2, [128, 8, 512], F32)
    xr = Ring(P, "fx", 2, [128, 512], F32)
    xnr = Ring(P, "fxn", 2, [128, 512], F32)
    psr = Ring(P, "dps", 3, [128, 512], F32, psum=True)
    for cb in range(2):
        cs = slice(cb * 512, (cb + 1) * 512)
        load_weight_bf(P, wd, Twd, W["w_down"][l][:, cs], 22, 512, st)
        for tt in range(NT):
            ts = slice(tt * 128, (tt + 1) * 128)
            xt, Tx = xr.next()
            xn, Txn = xnr.next()
            P.dma("sync", xt[:], C.xres[b][ts, cs], writes=[Tx], st=Tx)
            ps, Tps = psr.next()
            for kt in range(22):
                mm(P, ps[:], aT[:, kt, ts], wd[:, kt, :], kt == 0, kt == 21, [TaT[tt // 4], Twd], [Tps])
            P.op("vector", lambda e, xn=xn, ps=ps, xt=xt: e.tensor_add(out=xn[:], in0=ps[:], in1=xt[:]), reads=[Tps, Tx], writes=[Txn])
            P.dma("gpsimd", C.xres[b][ts, cs], xn[:], reads=[Txn], st=Txn)
    P.end_phase("ffn_down")
    P.pop()
    P.pop()


def phase_ple(P, C, l, b):
    W = C.W
    P.push()
    wg = P.sb("wg", [128, 8, D], BF16)
    Twg = Tile()
    wp = P.sb("wp", [128, 2, D], BF16)
    Twp = Tile()
    st = Ring(P, "pls", 2, [128, 8, 512], F32)
    for cb in range(2):
        cs = slice(cb * 512, (cb + 1) * 512)
        ws, Tws = st.next()
        P.dma("sync", ws[:], W["w_ple_gate"][l][:, cs].rearrange("(kt p) c -> p kt c", p=128), writes=[Tws], st=Tws)
        P.op("gpsimd", lambda e, ws=ws, cs=cs: e.tensor_copy(out=wg[:, :, cs], in_=ws[:]), reads=[Tws], writes=[Twg], acc=True)
    for cb in range(2):
        cs = slice(cb * 512, (cb + 1) * 512)
        ws, Tws = st.next()
        P.dma("sync", ws[:, 0:2, :], W["w_ple"][l][:, cs].rearrange("(kt p) c -> p kt c", p=128), writes=[Tws], st=Tws)
        P.op("gpsimd", lambda e, ws=ws, cs=cs: e.tensor_copy(out=wp[:, :, cs], in_=ws[:, 0:2, :]), reads=[Tws], writes=[Twp], acc=True)
    xr = Ring(P, "px", 2, [128, D], F32)
    xnr = Ring(P, "pxn", 2, [128, D], F32)
    pfr = Ring(P, "ppf", 2, [128, 256], F32)
    pbr = Ring(P, "ppb", 2, [128, 256], BF16)
    ptr = Ring(P, "ppt", 2, [128, 1024], BF16, psum=True)
    pTr = Ring(P, "ppT", 2, [128, 2, 128], BF16)
    psr = Ring(P, "pps", 4, [128, 512], F32, psum=True)
    sgr = Ring(P, "psg", 2, [128, 512], F32)
    prr = Ring(P, "ppr", 2, [128, 512], F32)
    for tt in range(NT):
        ts = slice(tt * 128, (tt + 1) * 128)
        xt, Tx = xr.next()
        xn, Txn = xnr.next()
        pf, Tpf = pfr.next()
        pb, Tpb = pbr.next()
        pt, Tpt = ptr.next()
        pT, TpT = pTr.next()
        P.dma("sync", xt[:], C.xres[b][ts, :], writes=[Tx], st=Tx)
        P.dma("sync", pf[:], C.p[l, b][ts, :], writes=[Tpf], st=Tpf)
        P.op("gpsimd", lambda e, pb=pb, pf=pf: e.tensor_copy(out=pb[:], in_=pf[:]), reads=[Tpf], writes=[Tpb])
        for kt in range(2):
            P.op("tensor", lambda e, pt=pt, pb=pb, kt=kt: e.transpose(pt[:, kt * 128:(kt + 1) * 128], pb[:, kt * 128:(kt + 1) * 128], C.ident),
                 reads=[Tpb, C.Tcbs], writes=[Tpt], acc=(kt > 0))
        P.op("scalar", lambda e, pT=pT, pt=pt: e.copy(out=pT[:].rearrange("p a b -> p (a b)"), in_=pt[:, 0:256]), reads=[Tpt], writes=[TpT])
        for cb in range(2):
            cs = slice(cb * 512, (cb + 1) * 512)
            pg, Tpg = psr.next()
            pe, Tpe = psr.next()
            for kt in range(8):
                mm(P, pg[:], C.hT[:, kt, ts], wg[:, kt, cs], kt == 0, kt == 7, [C.ThT[tt // 4], Twg], [Tpg])
            for kt in range(2):
                mm(P, pe[:], pT[:, kt, :], wp[:, kt, cs], kt == 0, kt == 1, [TpT, Twp], [Tpe])
            sg, Tsg = sgr.next()
            pr, Tpr = prr.next()
            P.op("scalar", lambda e, sg=sg, pg=pg: e.activation(out=sg[:], in_=pg[:], func=AF.Sigmoid), reads=[Tpg], writes=[Tsg])
            P.op("vector", lambda e, pr=pr, pe=pe, sg=sg: e.tensor_mul(out=pr[:], in0=pe[:], in1=sg[:]), reads=[Tpe, Tsg], writes=[Tpr])
            P.op("gpsimd", lambda e, xn=xn, xt=xt, pr=pr, cs=cs: e.tensor_add(out=xn[:, cs], in0=xt[:, cs], in1=pr[:]), reads=[Tx, Tpr], writes=[Txn],
                 acc=(cb > 0))
        P.dma("gpsimd", C.xres[b][ts, :], xn[:], reads=[Txn], st=Txn)
    P.end_phase("ple")
    P.pop()


def phase_final(P, C, b):
    P.push()
    gb = P.sb("fgb", [128, D], F32)
    Tgb = Tile()
    P.dma("sync", gb[:], C.W["final_norm_g"].partition_broadcast(128), writes=[Tgb], st=Tgb)
    xr = Ring(P, "fnx", 2, [128, D], F32)
    jr = Ring(P, "fnj", 2, [128, D], BF16)
    sr = Ring(P, "fns", 2, [128, 2], F32)
    orr = Ring(P, "fno", 2, [128, D], F32)
    for tt in range(NT):
        ts = slice(tt * 128, (tt + 1) * 128)
        xt, Tx = xr.next()
        jk, Tj = jr.next()
        ss, Tss = sr.next()
        o, To = orr.next()
        P.dma("sync", xt[:], C.xres[b][ts, :], writes=[Tx], st=Tx)
        P.op("scalar", lambda e, jk=jk, xt=xt, ss=ss: e.activation(out=jk[:], in_=xt[:], func=AF.Square, accum_out=ss[:, 0:1]), reads=[Tx], writes=[Tj, Tss])
        P.op("vector", lambda e, ss=ss: e.tensor_scalar(out=ss[:, 1:2], in0=ss[:, 0:1], scalar1=1.0 / D, scalar2=EPS, op0=ALU.mult, op1=ALU.add),
             reads=[Tss], writes=[Tss])
        P.op("scalar", lambda e, ss=ss: e.activation(out=ss[:, 1:2], in_=ss[:, 1:2], func=AF.Sqrt), reads=[Tss], writes=[Tss])
        P.op("vector", lambda e, ss=ss: e.reciprocal(out=ss[:, 1:2], in_=ss[:, 1:2]), reads=[Tss], writes=[Tss])
        P.op("vector", lambda e, o=o, xt=xt, ss=ss: e.scalar_tensor_tensor(out=o[:], in0=xt[:], scalar=ss[:, 1:2], in1=gb[:], op0=ALU.mult, op1=ALU.mult),
             reads=[Tx, Tss, Tgb], writes=[To])
        P.dma("gpsimd", C.y[b][ts, :], o[:], reads=[To], st=To)
    P.end_phase("final")
    P.pop()
```
